# Optimizing a Trainium2 kernel written in Bass

```python
import math
import jax
import jax.numpy as jnp
from jax import lax
import numpy as np

D_MODEL = 1024
BATCH = 32
SEQ = 2048
DEPTH = 4

RMS_EPS = 1e-6
D_FF = 2816

DN_HEADS = 4
DN_HEAD_DIM = 128
DN_WIDTH = DN_HEADS * DN_HEAD_DIM
DN_CONV = 5
DN_CHUNK = 64
L2_EPS = 1e-6

POOL_WINDOWS = (2, 4, 8, 16)
POOL_GROUPS = len(POOL_WINDOWS)
POOL_GROUP_DIM = 128
POOL_WIDTH = POOL_GROUPS * POOL_GROUP_DIM

DA_CONFIGS = ((128, 1), (512, 4), (2048, 16))
DA_NGROUPS = len(DA_CONFIGS)
DA_HEADS_PER_GROUP = 4
DA_HEAD_DIM = 64
DA_WIDTH = DA_NGROUPS * DA_HEADS_PER_GROUP * DA_HEAD_DIM
DA_OUT = DA_HEADS_PER_GROUP * DA_HEAD_DIM
DA_BLOCK = 64
ROPE_THETA = 10000.0
MASK_VALUE = -1e30

N_BRANCHES = 3

OFF_DN_QKV = 0
OFF_DN_Z = OFF_DN_QKV + 3 * DN_WIDTH
OFF_DN_BETA = OFF_DN_Z + DN_WIDTH
OFF_DN_A = OFF_DN_BETA + 2 * DN_HEADS
OFF_POOL = OFF_DN_A + 2 * DN_HEADS
OFF_DA = OFF_POOL + POOL_WIDTH
N_IN = OFF_DA + 3 * DA_WIDTH

kernel_name = 'hybrid_bidir_deltanet_pool_dilated_encoder'


def _rmsnorm(x, gain):
    xf = x.astype(jnp.float32)
    y = xf * lax.rsqrt(jnp.mean(xf * xf, axis=-1, keepdims=True) + RMS_EPS)
    return (y * gain.astype(jnp.float32)).astype(x.dtype)


def _l2norm(x):
    return x * lax.rsqrt(jnp.sum(x * x, axis=-1, keepdims=True) + L2_EPS)


def _swiglu(h, w_gate, w_up, w_down):
    return (jax.nn.silu(h @ w_gate) * (h @ w_up)) @ w_down


def _depthwise_conv_centred(x, w):
    K, C = w.shape
    return lax.conv_general_dilated(
        x, w[:, None, :], window_strides=(1,), padding=[(K // 2, K // 2)],
        dimension_numbers=('NWC', 'WIO', 'NWC'), feature_group_count=C)


def _gated_delta_chunked(q, k, v, g, beta):
    f32 = jnp.float32
    B_, H, S, Dk = q.shape
    Dv = v.shape[-1]
    C = DN_CHUNK
    N = S // C
    q = q.reshape(B_, H, N, C, Dk)
    k = k.reshape(B_, H, N, C, Dk)
    v = v.reshape(B_, H, N, C, Dv)
    beta = beta.reshape(B_, H, N, C)
    G = jnp.cumsum(g.reshape(B_, H, N, C), axis=-1)
    idx = jnp.arange(C)
    lower_incl = idx[:, None] >= idx[None, :]
    strict = idx[:, None] > idx[None, :]
    decay = jnp.exp(jnp.where(lower_incl, G[..., :, None] - G[..., None, :], -jnp.inf))
    kb = k * beta[..., None]
    kk = jnp.einsum('bhnid,bhnjd->bhnij', kb, k) * decay
    tri = jnp.where(strict, kk, 0.0) + jnp.eye(C, dtype=f32)
    rhs = jnp.concatenate([v * beta[..., None], kb * jnp.exp(G)[..., None]], axis=-1)
    sol = lax.linalg.triangular_solve(tri, rhs, left_side=True, lower=True, unit_diagonal=True)
    u, w = sol[..., :Dv], sol[..., Dv:]
    qk = jnp.where(lower_incl, jnp.einsum('bhnid,bhnjd->bhnij', q, k) * decay, 0.0)
    q_dec = q * jnp.exp(G)[..., None]
    k_dec = k * jnp.exp(G[..., -1:] - G)[..., None]
    g_last = jnp.exp(G[..., -1])

    def step(state, xs):
        qk_c, qd_c, kd_c, u_c, w_c, gl_c = xs
        v_new = u_c - jnp.einsum('bhck,bhkv->bhcv', w_c, state)
        o_c = (jnp.einsum('bhck,bhkv->bhcv', qd_c, state)
               + jnp.einsum('bhij,bhjv->bhiv', qk_c, v_new))
        state = state * gl_c[..., None, None] + jnp.einsum('bhck,bhcv->bhkv', kd_c, v_new)
        return state, o_c

    xs = tuple(jnp.moveaxis(t, 2, 0) for t in (qk, q_dec, k_dec, u, w, g_last))
    state0 = jnp.zeros((B_, H, Dk, Dv), f32)
    _, o = lax.scan(step, state0, xs)
    return jnp.moveaxis(o, 0, 2).reshape(B_, H, S, Dv)


def _deltanet_branch(qkv, z, beta_raw, a_raw, conv_w, a_log, dt_bias, out_norm):
    f32 = jnp.float32
    B_, S, _ = qkv.shape
    qkv = jax.nn.silu(_depthwise_conv_centred(qkv, conv_w)).astype(f32)

    def heads(t):
        return t.reshape(B_, S, DN_HEADS, DN_HEAD_DIM).transpose(0, 2, 1, 3)

    q = _l2norm(heads(qkv[..., :DN_WIDTH])) * (DN_HEAD_DIM ** -0.5)
    k = _l2norm(heads(qkv[..., DN_WIDTH:2 * DN_WIDTH]))
    v = heads(qkv[..., 2 * DN_WIDTH:])
    beta = jax.nn.sigmoid(beta_raw.astype(f32)).reshape(B_, S, 2, DN_HEADS).transpose(2, 0, 3, 1)
    g = (-jnp.exp(a_log.astype(f32))
         * jax.nn.softplus(a_raw.astype(f32).reshape(B_, S, 2, DN_HEADS) + dt_bias.astype(f32)))
    g = g.transpose(2, 0, 3, 1)
    o_fwd = _gated_delta_chunked(q, k, v, g[0], beta[0])
    flip = lambda t: jnp.flip(t, axis=2)
    o_bwd = flip(_gated_delta_chunked(flip(q), flip(k), flip(v), flip(g[1]), flip(beta[1])))
    o = (o_fwd + o_bwd).transpose(0, 2, 1, 3)
    o = _rmsnorm(o, out_norm) * jax.nn.silu(z.astype(f32).reshape(B_, S, DN_HEADS, DN_HEAD_DIM))
    return o.reshape(B_, S, DN_WIDTH).astype(z.dtype)


def _pooling_branch(u, pool_w, pool_scale):
    f32 = jnp.float32
    B_, S, _ = u.shape
    ug = u.astype(f32).reshape(B_, S, POOL_GROUPS, POOL_GROUP_DIM)
    csum = jnp.concatenate([jnp.zeros_like(ug[:, :1]), jnp.cumsum(ug, axis=1)], axis=1)
    pos = jnp.arange(S)
    outs = []
    for gi, win in enumerate(POOL_WINDOWS):
        lo = jnp.clip(pos - win // 2, 0, S)
        hi = jnp.clip(pos + (win - win // 2), 0, S)
        cnt = (hi - lo).astype(f32)
        cg = csum[:, :, gi]
        mean = (jnp.take(cg, hi, axis=1) - jnp.take(cg, lo, axis=1)) / cnt[None, :, None]
        outs.append(mean - ug[:, :, gi])
    pooled = jnp.stack(outs, axis=2)
    mixed = jnp.einsum('bsgc,gcd->bsgd', pooled, pool_w.astype(f32))
    return (mixed.reshape(B_, S, POOL_WIDTH) * pool_scale.astype(f32)).astype(u.dtype)


def _rope(x, pos):
    half = x.shape[-1] // 2
    inv_freq = ROPE_THETA ** (-jnp.arange(half, dtype=jnp.float32) / half)
    ang = pos.astype(jnp.float32)[:, None] * inv_freq[None, :]
    cos = jnp.cos(ang)[:, None, None, :]
    sin = jnp.sin(ang)[:, None, None, :]
    x1, x2 = x[..., :half], x[..., half:]
    return jnp.concatenate([x1 * cos - x2 * sin, x2 * cos + x1 * sin], axis=-1)


def _dilated_window_attention(q, k, v, dilation, radius):
    B_, S, H, Dh = q.shape
    L = S // dilation
    Q = DA_BLOCK
    nn = -(-radius // Q)
    nb = -(-L // Q)
    Lp = nb * Q

    def strided(t):
        return t.reshape(B_, L, dilation, H, Dh).transpose(0, 2, 1, 3, 4)

    qs, ks, vs = strided(q), strided(k), strided(v)
    qb = jnp.pad(qs, ((0, 0), (0, 0), (0, Lp - L), (0, 0), (0, 0))).reshape(B_, dilation, nb, Q, H, Dh)
    padk = ((0, 0), (0, 0), (nn * Q, Lp - L + nn * Q), (0, 0), (0, 0))
    kp = jnp.pad(ks, padk).reshape(B_, dilation, nb + 2 * nn, Q, H, Dh)
    vp = jnp.pad(vs, padk).reshape(B_, dilation, nb + 2 * nn, Q, H, Dh)
    kb = jnp.concatenate([kp[:, :, j:j + nb] for j in range(2 * nn + 1)], axis=3)
    vb = jnp.concatenate([vp[:, :, j:j + nb] for j in range(2 * nn + 1)], axis=3)
    s = jnp.einsum('brnqhd,brnkhd->brnhqk', qb, kb)
    blk = jnp.arange(nb)
    qpos = blk[:, None] * Q + jnp.arange(Q)[None, :]
    kpos = blk[:, None] * Q + jnp.arange((2 * nn + 1) * Q)[None, :] - nn * Q
    delta = kpos[:, None, :] - qpos[:, :, None]
    valid = (jnp.abs(delta) <= radius) & (kpos[:, None, :] >= 0) & (kpos[:, None, :] < L)
    s = jnp.where(valid[:, None], s, MASK_VALUE)
    lse = jax.nn.logsumexp(s, axis=-1)
    p = jnp.exp(s - lse[..., None])
    o = jnp.einsum('brnhqk,brnkhd->brnqhd', p, vb)
    o = o.reshape(B_, dilation, Lp, H, Dh)[:, :, :L].transpose(0, 2, 1, 3, 4).reshape(B_, S, H, Dh)
    lse = lse.transpose(0, 1, 2, 4, 3).reshape(B_, dilation, Lp, H)[:, :, :L]
    lse = lse.transpose(0, 2, 1, 3).reshape(B_, S, H)
    return o, lse


def _dilated_branch(qkv):
    f32 = jnp.float32
    B_, S, _ = qkv.shape
    t = qkv.astype(f32).reshape(B_, S, 3, DA_NGROUPS, DA_HEADS_PER_GROUP, DA_HEAD_DIM)
    pos = jnp.arange(S)
    q = _rope(t[:, :, 0], pos) * (DA_HEAD_DIM ** -0.5)
    k = _rope(t[:, :, 1], pos)
    v = t[:, :, 2]
    outs, lses = [], []
    for gi, (window, dil) in enumerate(DA_CONFIGS):
        o_g, lse_g = _dilated_window_attention(q[:, :, gi], k[:, :, gi], v[:, :, gi], dil, window // (2 * dil))
        outs.append(o_g)
        lses.append(lse_g)
    wts = jax.nn.softmax(jnp.stack(lses, axis=0), axis=0)
    merged = jnp.einsum('gbsh,gbshd->bshd', wts, jnp.stack(outs, axis=0))
    return merged.reshape(B_, S, DA_OUT).astype(qkv.dtype)


def setup_inputs(seed: int = 0) -> dict:
    key = jax.random.key(seed)
    ks = jax.random.split(key, 26)
    f32 = jnp.float32
    Lr, D, F = DEPTH, D_MODEL, D_FF

    def nrm(k_, shape, scale):
        return jax.random.normal(k_, shape, f32) * scale

    def gain(k_, shape):
        return 1.0 + 0.02 * jax.random.normal(k_, shape, f32)

    a_init = jax.random.uniform(ks[8], (Lr, 2, DN_HEADS), f32, 1.0, 16.0)
    dt = jnp.exp(jax.random.uniform(ks[9], (Lr, 2, DN_HEADS), f32, math.log(1e-3), math.log(1e-1)))
    return {
        'x': jax.random.normal(ks[0], (BATCH, SEQ, D), f32),
        'ffn1_norm': gain(ks[1], (Lr, D)),
        'ffn1_w_gate': nrm(ks[2], (Lr, D, F), D ** -0.5),
        'ffn1_w_up': nrm(ks[3], (Lr, D, F), D ** -0.5),
        'ffn1_w_down': nrm(ks[4], (Lr, F, D), F ** -0.5),
        'mix_norm': gain(ks[5], (Lr, D)),
        'w_in': nrm(ks[6], (Lr, D, N_IN), D ** -0.5),
        'dn_conv': nrm(ks[7], (Lr, DN_CONV, 3 * DN_WIDTH), DN_CONV ** -0.5),
        'dn_a_log': jnp.log(a_init),
        'dn_dt_bias': dt + jnp.log(-jnp.expm1(-dt)),
        'dn_out_norm': gain(ks[10], (Lr, DN_HEAD_DIM)),
        'pool_w': nrm(ks[11], (Lr, POOL_GROUPS, POOL_GROUP_DIM, POOL_GROUP_DIM), POOL_GROUP_DIM ** -0.5),
        'pool_scale': gain(ks[12], (Lr, POOL_WIDTH)),
        'w_proj_a': nrm(ks[13], (Lr, DN_WIDTH, D), DN_WIDTH ** -0.5),
        'w_proj_b': nrm(ks[14], (Lr, POOL_WIDTH, D), POOL_WIDTH ** -0.5),
        'w_proj_c': nrm(ks[15], (Lr, DA_OUT, D), DA_OUT ** -0.5),
        'w_gate': nrm(ks[16], (Lr, D, N_BRANCHES * D), D ** -0.5),
        'b_gate': nrm(ks[17], (Lr, N_BRANCHES * D), 0.01),
        'w_out': nrm(ks[18], (Lr, D, D), D ** -0.5),
        'ffn2_norm': gain(ks[19], (Lr, D)),
        'ffn2_w_gate': nrm(ks[20], (Lr, D, F), D ** -0.5),
        'ffn2_w_up': nrm(ks[21], (Lr, D, F), D ** -0.5),
        'ffn2_w_down': nrm(ks[22], (Lr, F, D), F ** -0.5),
        'final_norm': gain(ks[23], (D,)),
    }


def reference(x, ffn1_norm, ffn1_w_gate, ffn1_w_up, ffn1_w_down, mix_norm, w_in, dn_conv, dn_a_log,
              dn_dt_bias, dn_out_norm, pool_w, pool_scale, w_proj_a, w_proj_b, w_proj_c, w_gate, b_gate,
              w_out, ffn2_norm, ffn2_w_gate, ffn2_w_up, ffn2_w_down, final_norm):
    B_, S, D = x.shape
    for l in range(DEPTH):
        x = x + 0.5 * _swiglu(_rmsnorm(x, ffn1_norm[l]), ffn1_w_gate[l], ffn1_w_up[l], ffn1_w_down[l])
        h = _rmsnorm(x, mix_norm[l])
        proj = h @ w_in[l]
        y_a = _deltanet_branch(proj[..., OFF_DN_QKV:OFF_DN_Z], proj[..., OFF_DN_Z:OFF_DN_BETA],
                               proj[..., OFF_DN_BETA:OFF_DN_A], proj[..., OFF_DN_A:OFF_POOL],
                               dn_conv[l], dn_a_log[l], dn_dt_bias[l], dn_out_norm[l]) @ w_proj_a[l]
        y_b = _pooling_branch(proj[..., OFF_POOL:OFF_DA], pool_w[l], pool_scale[l]) @ w_proj_b[l]
        y_c = _dilated_branch(proj[..., OFF_DA:N_IN]) @ w_proj_c[l]
        gates = jax.nn.sigmoid(h @ w_gate[l] + b_gate[l]).reshape(B_, S, N_BRANCHES, D)
        merged = gates[:, :, 0] * y_a + gates[:, :, 1] * y_b + gates[:, :, 2] * y_c
        x = x + merged @ w_out[l]
        x = x + 0.5 * _swiglu(_rmsnorm(x, ffn2_norm[l]), ffn2_w_gate[l], ffn2_w_up[l], ffn2_w_down[l])
    return _rmsnorm(x, final_norm)
```

```python
import math
import numpy as np
import concourse.bass as bass
import concourse.mybir as mybir
from concourse.bass_utils import run_bass_kernel_spmd
from contextlib import ExitStack

F32 = mybir.dt.float32
BF16 = mybir.dt.bfloat16
U8 = mybir.dt.uint8
AF = mybir.ActivationFunctionType
ALU = mybir.AluOpType
AX = mybir.AxisListType

D = 1024
S = 2048
DFF = 2816
NL = 4
NCORES = 8
NSEQ = 4
RMS_EPS = 1e-6
L2_EPS = 1e-6
NF = DFF // 128
NT = S // 512
NB = S // 128

ENGS = ("pe", "act", "dve", "pool", "sp")
STRICT_SAME_ENGINE = True


class Slot:
    __slots__ = ("name", "last_w", "readers", "sem", "sem_cnt", "excl")

    def __init__(self, name, excl=False):
        self.name = name
        self.excl = excl
        self.last_w = None
        self.readers = {}
        self.sem = None
        self.sem_cnt = 0


class Ins:
    __slots__ = ("eng", "idx", "fn", "deps", "awaited", "count", "dma", "sem", "sem_target")

    def __init__(self, eng, idx, fn, dma):
        self.eng = eng
        self.idx = idx
        self.fn = fn
        self.deps = []
        self.awaited = False
        self.count = 0
        self.dma = dma
        self.sem = None
        self.sem_target = 0


class Prog:
    def __init__(self, nc):
        self.nc = nc
        self.streams = {e: [] for e in ENGS}
        self.es = ExitStack()
        self.dma_slots = []
        self.last_dma = {}

    def sbuf(self, name, shape, dt):
        return self.es.enter_context(self.nc.sbuf_tensor(name, list(shape), dt))

    def psum(self, name, shape, dt):
        return self.es.enter_context(self.nc.psum_tensor(name, list(shape), dt))

    def new_sem(self, name):
        return self.es.enter_context(self.nc.semaphore(name))

    def op(self, eng, fn, reads=(), writes=(), dma=False, dma_slot=None, extra_deps=()):
        st = self.streams[eng]
        ins = Ins(eng, len(st), fn, dma)
        deps = list(extra_deps)
        for s in reads:
            if s.last_w is not None:
                deps.append(s.last_w)
            if s.excl:
                deps.extend(r for kk_, r in s.readers.items() if kk_ != eng)
        for s in writes:
            if s.last_w is not None:
                deps.append(s.last_w)
            deps.extend(s.readers.values())
        seen = set()
        for d in deps:
            if d is ins or id(d) in seen:
                continue
            seen.add(id(d))
            if (not d.dma) and d.eng == eng and (eng == 'pe' or not STRICT_SAME_ENGINE):
                continue
            ins.deps.append(d)
            d.awaited = True
        for s in reads:
            key = eng if not dma else ("dma", eng, ins.idx)
            s.readers[key] = ins
        for s in writes:
            s.last_w = ins
            s.readers = {}
        if dma:
            if dma_slot.sem is None:
                dma_slot.sem = self.new_sem("d_" + dma_slot.name)
                self.dma_slots.append(dma_slot)
            dma_slot.sem_cnt += 16
            ins.sem = dma_slot.sem
            ins.sem_target = dma_slot.sem_cnt
            ins.awaited = True
            self.last_dma[id(dma_slot)] = ins
        st.append(ins)
        return ins

    def barrier(self):
        extra = list(self.last_dma.values())
        for e in ENGS:
            if e != "sp" and self.streams[e]:
                extra.append(self.streams[e][-1])
        a = self.op("sp", lambda eng: eng.nop(), extra_deps=extra)
        for e in ENGS:
            if e != "sp":
                self.op(e, lambda eng: eng.nop(), extra_deps=[a])

    def emit(self):
        nc = self.nc
        esems = {e: self.new_sem("e_" + e) for e in ENGS}
        for e in ENGS:
            c = 0
            for ins in self.streams[e]:
                if ins.awaited and not ins.dma:
                    c += 1
                ins.count = c
        final_waits = [(s.sem, s.sem_cnt) for s in self.dma_slots]
        streams = self.streams
        stats = {}

        def run(e, eng):
            known = {}
            nw = 0
            for ins in streams[e]:
                for d in ins.deps:
                    if d.dma:
                        k = ("s", id(d.sem))
                        if known.get(k, 0) >= d.sem_target:
                            continue
                        known[k] = d.sem_target
                        eng.wait_ge(d.sem, d.sem_target)
                        nw += 1
                    else:
                        if known.get(d.eng, -1) >= d.idx:
                            continue
                        known[d.eng] = d.idx
                        eng.wait_ge(esems[d.eng], d.count)
                        nw += 1
                bi = ins.fn(eng)
                if ins.dma:
                    bi.then_inc(ins.sem, 16)
                elif ins.awaited:
                    bi.then_inc(esems[e], 1)
            if e == "sp":
                for sem, v in final_waits:
                    eng.wait_ge(sem, v)
            stats[e] = (len(streams[e]), nw)

        with nc.Block() as block:
            @block.sync
            def _(eng):
                run("sp", eng)

            @block.tensor
            def _(eng):
                run("pe", eng)

            @block.scalar
            def _(eng):
                run("act", eng)

            @block.vector
            def _(eng):
                run("dve", eng)

            @block.gpsimd
            def _(eng):
                run("pool", eng)
        self.stats = stats
        self.es.close()


OFF_DN_Z = 1536
OFF_DN_BA = 2048
OFF_POOL = 2064
OFF_DA = 2576


def _chunk_table():
    t = []

    def add(key, mat, c0, nc_, K, grp):
        t.append(dict(key=key, mat=mat, c0=c0, n=nc_, K=K, grp=grp))

    for f in ("ffn1", "ffn2"):
        grp = 0 if f == "ffn1" else 2
        for i in range(NF):
            add((f, "g", i), f + "_w_gate", i * 128, 128, D, grp)
            add((f, "u", i), f + "_w_up", i * 128, 128, D, grp)
        for d in range(8):
            add((f, "d", d), f + "_w_down", d * 128, 128, DFF, grp)
        if f == "ffn1":
            for h in range(4):
                add(("dn", "q", h), "w_in", h * 128, 128, D, 1)
                add(("dn", "k", h), "w_in", 512 + h * 128, 128, D, 1)
                add(("dn", "v", h), "w_in", 1024 + h * 128, 128, D, 1)
                add(("dn", "z", h), "w_in", OFF_DN_Z + h * 128, 128, D, 1)
            add(("dn", "ba"), "w_in", OFF_DN_BA, 16, D, 1)
            for g in range(4):
                add(("pool", "u", g), "w_in", OFF_POOL + g * 128, 128, D, 1)
                add(("pool", "w", g), "pool_w", g, 128, 128, 1)
            for t3, nm in enumerate("qkv"):
                for c in range(6):
                    add(("da", nm, c), "w_in", OFF_DA + t3 * 768 + c * 128, 128, D, 1)
            for d in range(8):
                add(("pa", d), "w_proj_a", d * 128, 128, 512, 1)
                add(("pb", d), "w_proj_b", d * 128, 128, 512, 1)
                add(("pc", d), "w_proj_c", d * 128, 128, 256, 1)
                add(("wo", d), "w_out", d * 128, 128, D, 1)
            for i in range(24):
                add(("gate", i), "w_gate", i * 128, 128, D, 1)
    off = 0
    tab = {}
    grp_rng = {}
    for c in t:
        sz = 128 * (c["K"] // 128) * c["n"]
        c["off"] = off
        c["sz"] = sz
        tab[c["key"]] = c
        g = c["grp"]
        lo, hi = grp_rng.get(g, (off, off))
        grp_rng[g] = (min(lo, off), off + sz)
        off += sz
    return t, tab, off, grp_rng


CHUNKS, CHTAB, WTOT, GRP_RNG = _chunk_table()
WROW = 4096
WTOT_PAD = ((WTOT + WROW * 16 - 1) // (WROW * 16)) * (WROW * 16)

PP = {}
_o = 0
for _n, _w in (("ffn1_norm", 8), ("mix_norm", 8), ("ffn2_norm", 8), ("b_gate", 24), ("pool_scale", 4),
               ("conv", 60), ("out_norm", 128), ("a_log", 8), ("dt_bias", 8)):
    PP[_n] = (_o, _w)
    _o += _w
NPP = _o


def _host_weights(inp):
    wf = np.zeros((NL, WTOT_PAD), np.float32)
    for l in range(NL):
        for c in CHUNKS:
            m = inp[c["mat"]][l]
            if c["mat"] == "pool_w":
                blk = m[c["c0"]]
            else:
                blk = m[:, c["c0"]:c["c0"] + c["n"]]
            K = c["K"]
            arr = blk.reshape(K // 128, 128, c["n"]).transpose(1, 0, 2)
            wf[l, c["off"]:c["off"] + c["sz"]] = arr.reshape(-1)
    return wf


def _host_params(inp):
    pp = np.zeros((NL, 128, NPP), np.float32)
    for l in range(NL):
        def put(name, arr):
            o, w = PP[name]
            pp[l, :, o:o + w] = arr
        put("ffn1_norm", inp["ffn1_norm"][l].reshape(8, 128).T)
        put("mix_norm", inp["mix_norm"][l].reshape(8, 128).T)
        put("ffn2_norm", inp["ffn2_norm"][l].reshape(8, 128).T)
        put("b_gate", inp["b_gate"][l].reshape(24, 128).T)
        put("pool_scale", inp["pool_scale"][l].reshape(4, 128).T)
        cv = inp["dn_conv"][l].reshape(5, 3, 4, 128).transpose(3, 1, 2, 0).reshape(128, 60)
        put("conv", cv)
        put("out_norm", np.broadcast_to(inp["dn_out_norm"][l][None, :], (128, 128)))
        put("a_log", np.broadcast_to(inp["dn_a_log"][l].reshape(1, 8), (128, 8)))
        put("dt_bias", np.broadcast_to(inp["dn_dt_bias"][l].reshape(1, 8), (128, 8)))
    fin = np.ascontiguousarray(inp["final_norm"].reshape(8, 128).T)
    return pp, fin


DA_CFG = ((128, 1), (512, 4), (2048, 16))
CF = {}
_o = 0
for _n, _w in (("ones", 128), ("negones", 128), ("Uf", 128), ("Ub", 128), ("NEGf", 128), ("NEGb", 128), ("pedge", 64)):
    CF[_n] = (_o, _w)
    _o += _w
NCF = _o
CB = {}
_o = 0
for _n, _w in (("ident", 128), ("MU", 512), ("ML", 512), ("PERM", 128), ("BAND", 256), ("OP0", 128), ("OP1", 128), ("ones", 128)):
    CB[_n] = (_o, _w)
    _o += _w
NCB = _o


def _host_consts():
    cf = np.zeros((128, NCF), np.float32)
    cb = np.zeros((128, NCB), np.float32)
    ii = np.arange(128)
    P_, F_ = ii[:, None], ii[None, :]

    def putf(n, a):
        o, w = CF[n]
        cf[:, o:o + w] = a

    def putb(n, a):
        o, w = CB[n]
        cb[:, o:o + w] = a
    putf("ones", 1.0)
    putf("negones", -1.0)
    putf("Uf", (P_ <= F_).astype(np.float32))
    putf("Ub", (P_ >= F_).astype(np.float32))
    putf("NEGf", np.where(F_ >= P_, 0.0, -30000.0))
    putf("NEGb", np.where(F_ <= P_, 0.0, -30000.0))
    pe = np.zeros((4, 16), np.float32)
    for gi, w in enumerate((2, 4, 8, 16)):
        for t in range(8):
            lo = max(t - w // 2, 0); hi = min(t + (w - w // 2), S)
            pe[gi, t] = 1.0 / (hi - lo)
            tt = S - 8 + t
            lo = max(tt - w // 2, 0); hi = min(tt + (w - w // 2), S)
            pe[gi, 8 + t] = 1.0 / (hi - lo)
    putf("pedge", np.broadcast_to(pe.reshape(1, 64), (128, 64)))
    putb("ident", np.eye(128))
    mu = np.zeros((4, 128, 128), np.float32)
    for l in range(4):
        big = 16 << l
        m = (F_ > P_) & (F_ // big == P_ // big)
        if l > 0:
            m &= (F_ // (big // 2) != P_ // (big // 2))
        mu[l] = m
    putb("MU", mu.transpose(1, 0, 2).reshape(128, 512))
    putb("ML", mu.transpose(2, 0, 1).reshape(128, 512))
    perm = np.zeros((128, 128), np.float32)
    for m in range(128):
        hh, d = divmod(m, 64)
        perm[hh * 64 + (d + 32) % 64, m] = 1.0
    putb("PERM", perm)
    a = ii[:, None]; b = np.arange(256)[None, :]
    putb("BAND", ((a <= b) & (b <= a + 128)).astype(np.float32))
    op0 = np.zeros((128, 128), np.float32); op0[:, :64] = 1.0
    op1 = np.zeros((128, 128), np.float32); op1[:, 64:] = 1.0
    putb("OP0", op0); putb("OP1", op1)
    putb("ones", 1.0)
    rope = np.zeros((3, 2, 128, S), np.float32)
    half = 32
    inv_freq = (10000.0 ** (-np.arange(half, dtype=np.float32) / half)).astype(np.float32)
    d = np.arange(128) % 64
    fr = inv_freq[d % 32]
    sgn = np.where(d < 32, -1.0, 1.0).astype(np.float32)
    for gi, (_, dil) in enumerate(DA_CFG):
        L = S // dil
        n = np.arange(S)
        tok = (n % L) * dil + (n // L)
        ang = tok.astype(np.float32)[None, :] * fr[:, None]
        rope[gi, 0] = np.cos(ang)
        rope[gi, 1] = np.sin(ang) * sgn[:, None]
    return cf, cb, rope


class K:
    def __init__(self, nseq, nlayers, stages, debug=None):
        self.nseq = nseq
        self.nlayers = nlayers
        self.stages = stages
        self.debug = debug
        nc = bass.Bass("TRN2", target_bir_lowering=False)
        self.nc = nc
        self.P = Prog(nc)
        P = self.P
        self.x_in = nc.dram_tensor("x_in", [nseq, 128, 8 * S], F32, kind="ExternalInput").ap()
        self.out = nc.dram_tensor("out", [nseq, 128, 8 * S], F32, kind="ExternalOutput").ap()
        self.wf = nc.dram_tensor("wf", [NL, WTOT_PAD // WROW, WROW], F32, kind="ExternalInput").ap()
        self.wb = nc.dram_tensor("wb", [NL, WTOT_PAD // WROW, WROW], BF16, kind="Internal").ap()
        self.pp_d = nc.dram_tensor("pp", [128, NL * NPP], F32, kind="ExternalInput").ap()
        self.fin_d = nc.dram_tensor("fin", [128, 8], F32, kind="ExternalInput").ap()
        self.cf_d = nc.dram_tensor("cf", [128, NCF], F32, kind="ExternalInput").ap()
        self.cb_d = nc.dram_tensor("cb", [128, NCB], F32, kind="ExternalInput").ap()
        self.rope_d = nc.dram_tensor("rope", [3, 2, 128, S], F32, kind="ExternalInput").ap()
        self.xs_d = nc.dram_tensor("xs", [128, 8 * S], F32, kind="Internal").ap()
        self.wb_flat = self.wb.rearrange("l r c -> l (r c)")
        self.s_wb = {}
        self.arena_bytes = 212000
        self.arena = P.sbuf("arena", [128, self.arena_bytes], U8)
        self.psums = [(P.psum(f"ps{i}", [128, 512], F32), Slot(f"ps{i}", excl=True)) for i in range(8)]
        self.ps_i = 0

    def view(self, off, n, dt):
        sz = 4 if dt == F32 else 2
        assert off % 4 == 0 and off + n * sz <= self.arena_bytes, (off, n, sz)
        return self.arena[:, off:off + n * sz].bitcast(dt)

    def dump(self, name, ap, slot, shape, dt):
        if self.debug is None or name not in self.debug:
            return
        d = self.nc.dram_tensor("dbg_" + name, list(shape), dt, kind="ExternalOutput").ap()
        ds = Slot("dbg_" + name)
        self.P.op("sp", lambda e: e.dma_start(out=d, in_=ap), reads=[slot], dma=True, dma_slot=ds)

    def next_ps(self):
        r = self.psums[self.ps_i % 8]
        self.ps_i += 1
        return r


class Alloc:
    def __init__(self, k, start, end):
        self.k, self.o, self.end = k, start, end

    def __call__(self, name, n, dt, shape3=None):
        sz = 4 if dt == F32 else 2
        self.o = (self.o + 3) // 4 * 4
        v = self.k.view(self.o, n, dt)
        self.o += n * sz
        assert self.o <= self.end, (name, self.o, self.end)
        cache = self.k.__dict__.setdefault("slot_cache", {})
        if name not in cache:
            cache[name] = Slot(name)
        return v, cache[name]


def build(nseq=NSEQ, nlayers=NL, stages=("ffn1", "mix", "ffn2"), debug=None):
    k = K(nseq, nlayers, stages, debug)
    nc, P = k.nc, k.P
    KB = 1024

    def mm(out, lhsT, rhs, start, stop, reads, writes):
        return P.op("pe", lambda e: e.matmul(out, lhsT=lhsT, rhs=rhs, start=start, stop=stop), reads=reads, writes=writes)

    def act(out, in_, func, reads, writes, bias=None, scale=None, eng="act"):
        kw = {}
        if bias is not None:
            kw["bias"] = bias
        if scale is not None:
            kw["scale"] = scale
        return P.op("act", lambda e: e.activation(out=out, in_=in_, func=func, **kw), reads=reads, writes=writes)

    def tt(eng, out, in0, in1, op, reads, writes):
        return P.op(eng, lambda e: e.tensor_tensor(out=out, in0=in0, in1=in1, op=op), reads=reads, writes=writes)

    def stt(out, in0, scalar, in1, op0, op1, reads, writes):
        return P.op("dve", lambda e: e.scalar_tensor_tensor(out=out, in0=in0, scalar=scalar, in1=in1, op0=op0, op1=op1),
                    reads=reads, writes=writes)

    def ts(eng, out, in0, s1, op0, reads, writes, s2=None, op1=None):
        if op1 is None:
            return P.op(eng, lambda e: e.tensor_scalar(out=out, in0=in0, scalar1=s1, scalar2=None, op0=op0), reads=reads, writes=writes)
        return P.op(eng, lambda e: e.tensor_scalar(out=out, in0=in0, scalar1=s1, scalar2=s2, op0=op0, op1=op1), reads=reads, writes=writes)

    def cp(eng, out, in_, reads, writes):
        if eng == "act":
            return P.op("act", lambda e: e.activation(out=out, in_=in_, func=AF.Copy), reads=reads, writes=writes)
        return P.op(eng, lambda e: e.tensor_copy(out=out, in_=in_), reads=reads, writes=writes)

    def memset(eng, out, val, writes):
        return P.op(eng, lambda e: e.memset(out, val), writes=writes)

    def dma(out, in_, reads, writes, slot, eng="sp"):
        return P.op(eng, lambda e: e.dma_start(out=out, in_=in_), reads=reads, writes=writes, dma=True, dma_slot=slot)

    XR = 0
    pa = Alloc(k, 64 * KB, k.arena_bytes)
    pp_t, s_pp = pa("pp", NL * NPP, F32)
    fin_t, s_fin = pa("fin", 8, F32)
    cf_t, s_cf = pa("cf", NCF, F32)
    cb_t, s_cb = pa("cb", NCB, BF16)
    eps_t, s_const = pa("eps", 8, F32)
    NRING = 8
    ring = [pa(f"ring{i}", 1024, BF16) for i in range(NRING)]
    sq_flat, s_sq = pa("sq", 8 * 512, BF16)
    sq_t = sq_flat.rearrange("p (c t) -> p c t", c=8)
    rstd_t, s_rstd = pa("rstd", 512, F32)
    PH = (pa.o + 31) // 32 * 32
    xT = k.view(XR, 8 * S, F32).rearrange("p (c t) -> p c t", c=8)
    s_x = [Slot(f"x{j}") for j in range(NT)]
    s_xs = [Slot(f"xs{j}") for j in range(NT)]

    def cfv(n):
        o_, w_ = CF[n]
        return cf_t[:, o_:o_ + w_]

    def cbv(n, i=None, w=128):
        o_, w_ = CB[n]
        if i is None:
            return cb_t[:, o_:o_ + w_]
        return cb_t[:, o_ + i * w:o_ + (i + 1) * w]

    onesb = cbv("ones")
    ident = cbv("ident")

    fa = Alloc(k, PH, k.arena_bytes)
    NDR = 3
    dring = [fa(f"dring{i}", NF * 128, BF16) for i in range(NDR)]
    hT_flat, s_hT = fa("hTt", 8 * 512, BF16)
    hT_t = hT_flat.rearrange("p (c t) -> p c t", c=8)
    act_flat, _ = fa("act", NF * 512, BF16)
    act_t = act_flat.rearrange("p (c t) -> p c t", c=NF)
    s_act = [Slot(f"act{f}") for f in range(NF)]
    sg_t = [fa(f"sg{i}", 512, F32) for i in range(2)]

    ma = Alloc(k, PH, k.arena_bytes)
    hTf_flat, s_hTf = ma("hTf", 8 * S, BF16)
    hT_full = hTf_flat.rearrange("p (c t) -> p c t", c=8)
    ya_flat, s_ya = ma("ya", 4 * S, BF16)
    y_a = ya_flat.rearrange("p (c t) -> p c t", c=4)
    EXT0 = ma.o
    yb_flat, s_yb = ma("yb", 4 * S, BF16)
    y_b = yb_flat.rearrange("p (c t) -> p c t", c=4)
    yc_flat, s_yc = ma("yc", 2 * S, BF16)
    y_c = yc_flat.rearrange("p (c t) -> p c t", c=2)
    MRG0 = ma.o
    m32_flat, s_m32 = ma("m32", 8 * 512, F32)
    m32 = m32_flat.rearrange("p (c t) -> p c t", c=8)
    mb_flat, s_mb = ma("mb", 8 * 512, BF16)
    mb = mb_flat.rearrange("p (c t) -> p c t", c=8)
    gate_t = [ma(f"gate{i}", 512, F32) for i in range(2)]
    gt2_t = [ma(f"gt2{i}", 512, F32) for i in range(2)]
    EXT1 = ma.o

    ring_i = [0]
    dring_i = [0]

    ROWS_PER = 256
    nrows = WTOT_PAD // WROW
    for l in range(nlayers):
        k.s_wb[l] = Slot(f"wb{l}")
    for l in range(nlayers):
        for r0 in range(0, nrows, ROWS_PER):
            r1 = min(nrows, r0 + ROWS_PER)
            if r0 * WROW >= WTOT:
                break
            dma(k.wb[l, r0:r1, :], k.wf[l, r0:r1, :], [], [k.s_wb[l]], k.s_wb[l], eng="pool")

    def load_chunk(l, key):
        c = CHTAB[key]
        if c["K"] == DFF:
            t, s = dring[dring_i[0] % NDR]
            dring_i[0] += 1
        else:
            t, s = ring[ring_i[0] % NRING]
            ring_i[0] += 1
        n = c["sz"] // 128
        src = k.wb_flat[l, c["off"]:c["off"] + c["sz"]].rearrange("(p f) -> p f", p=128)
        dma(t[:, 0:n], src, [k.s_wb[l]], [s], s)
        return t[:, 0:n].rearrange("p (k c) -> p k c", c=c["n"]), s

    dma(pp_t, k.pp_d, [], [s_pp], s_pp)
    dma(fin_t, k.fin_d, [], [s_fin], s_fin)
    dma(cf_t, k.cf_d, [], [s_cf], s_cf)
    dma(cb_t, k.cb_d, [], [s_cb], s_cb, eng="pool")
    for i, v in enumerate((RMS_EPS, L2_EPS, 0.5, 1.0, -1.0, 128.0 ** -0.5)):
        memset("pool", eps_t[:, i:i + 1], v, [s_const])
    c_eps, c_l2eps, c_half, c_one, c_neg1, c_qs = [eps_t[:, i:i + 1] for i in range(6)]

    def ppv(l, name, j=None, w=1):
        o_, w_ = PP[name]
        base = l * NPP + o_
        if j is None:
            return pp_t[:, base:base + w_]
        return pp_t[:, base + j:base + j + w]

    def rmsnorm_tile(j, gain_fn, gain_slot, out_ap, out_slot):
        cols = slice(j * 512, (j + 1) * 512)
        act(sq_t, xT[:, :, cols], AF.Square, [s_x[j]], [s_sq])
        ps, sps = k.next_ps()
        for c in range(8):
            mm(ps[:], onesb, sq_t[:, c, :], c == 0, c == 7, [s_cb, s_sq], [sps])
        act(rstd_t, ps[:], AF.Sqrt, [sps, s_const], [s_rstd], bias=c_eps, scale=1.0 / D)
        P.op("dve", lambda e: e.reciprocal(out=rstd_t, in_=rstd_t), reads=[s_rstd], writes=[s_rstd])
        for c in range(8):
            stt(out_ap[:, c, :], xT[:, c, cols], gain_fn(c), rstd_t, ALU.mult, ALU.mult, [s_x[j], gain_slot, s_rstd], [out_slot])

    def ffn(l, which):
        nm = which + "_norm"
        for j in range(NT):
            cols = slice(j * 512, (j + 1) * 512)
            rmsnorm_tile(j, lambda c: ppv(l, nm, c), s_pp, hT_t, s_hT)
            for f in range(NF):
                wg, swg = load_chunk(l, (which, "g", f))
                wu, swu = load_chunk(l, (which, "u", f))
                pg, spg = k.next_ps()
                pu, spu = k.next_ps()
                for c in range(8):
                    mm(pg[:], wg[:, c, :], hT_t[:, c, :], c == 0, c == 7, [swg, s_hT], [spg])
                for c in range(8):
                    mm(pu[:], wu[:, c, :], hT_t[:, c, :], c == 0, c == 7, [swu, s_hT], [spu])
                sg, ssg = sg_t[f % 2]
                act(sg, pg[:], AF.Silu, [spg], [ssg])
                tt("dve", act_t[:, f, :], pu[:], sg, ALU.mult, [spu, ssg], [s_act[f]])
            for d in range(8):
                wd, swd = load_chunk(l, (which, "d", d))
                pd, spd = k.next_ps()
                for f in range(NF):
                    mm(pd[:], wd[:, f, :], act_t[:, f, :], f == 0, f == NF - 1, [swd, s_act[f]], [spd])
                stt(xT[:, d, cols], pd[:], c_half, xT[:, d, cols], ALU.mult, ALU.add, [spd, s_x[j], s_const], [s_x[j]])

    def deltanet(l):
        xa = Alloc(k, XR, 64 * KB)
        beta_all, s_beta = xa("beta", 128, F32)
        g_all, s_g = xa("g", 128, F32)
        G_all, s_G = xa("G", 128, F32)
        eG, s_eG = xa("eG", 128, F32)
        negeG, s_negeG = xa("negeG", 128, F32)
        edec, s_edec = xa("edec", 128, F32)
        egl, s_egl = xa("egl", 128, F32)
        t_a, s_ta = xa("ta", 128, F32)
        negA, s_negA = xa("negA", 8, F32)
        v3 = lambda t: t.rearrange("p (b c) -> p b c", c=8)
        qT, s_qT = xa("qT", S, BF16)
        kT, s_kT = xa("kT", S, BF16)
        vtok_f, s_vtok = xa("vtok", S, BF16)
        v_tok = vtok_f.rearrange("p (b c) -> p b c", c=128)
        kdec = []
        for d_ in range(2):
            t_, s_ = xa(f"kdec{d_}", S, BF16)
            kdec.append((t_.rearrange("p (b c) -> p b c", c=128), s_))
        oacc_f, s_oacc = xa("oacc", S, F32)
        o_acc = oacc_f.rearrange("p (b c) -> p b c", c=128)
        Wall, QKTa, qdTa = [], [], []
        for d_ in range(2):
            for lst, nm in ((Wall, "W"), (QKTa, "QKT"), (qdTa, "qdT")):
                t_, s_ = xa(f"{nm}{d_}", S, BF16)
                lst.append((t_, [Slot(f"{nm}{d_}_{g}") for g in range(4)]))
        St = [xa(f"S{d_}", 128, F32) for d_ in range(2)]
        Sb = [xa(f"Sb{d_}", 128, BF16) for d_ in range(2)]
        Rt = [xa(f"R{d_}", 128, BF16) for d_ in range(2)]
        Vn = [xa(f"Vn{d_}", 128, BF16) for d_ in range(2)]

        ba, sba = load_chunk(l, ("dn", "ba"))
        ps, sps = k.next_ps()
        psv = ps[:, 0:256].rearrange("p (b c) -> p b c", c=16)
        for b in range(NB):
            for c in range(8):
                mm(ps[:, b * 16:(b + 1) * 16], hT_full[:, c, b * 128:(b + 1) * 128], ba[:, c, :], c == 0, c == 7, [s_hTf, sba], [sps])
        act(v3(beta_all), psv[:, :, 0:8], AF.Sigmoid, [sps], [s_beta])
        dtb = ppv(l, "dt_bias").unsqueeze(1).to_broadcast([128, NB, 8])
        tt("dve", v3(t_a), psv[:, :, 8:16], dtb, ALU.add, [sps, s_pp], [s_ta])
        act(t_a, t_a, AF.Exp, [s_ta], [s_ta])
        act(t_a, t_a, AF.Ln, [s_ta, s_const], [s_ta], bias=c_one)
        act(negA, ppv(l, "a_log"), AF.Exp, [s_pp], [s_negA])
        ts("dve", negA, negA, -1.0, ALU.mult, [s_negA], [s_negA])
        tt("dve", v3(g_all), v3(t_a), negA.unsqueeze(1).to_broadcast([128, NB, 8]), ALU.mult, [s_ta, s_negA], [s_g])
        psF, spsF = k.next_ps()
        mm(psF[:, 0:128], cfv("Uf"), g_all, True, True, [s_cf, s_g], [spsF])
        mm(psF[:, 128:256], cfv("Ub"), g_all, True, True, [s_cf, s_g], [spsF])
        mm(psF[:, 256:384], cfv("ones"), g_all, True, True, [s_cf, s_g], [spsF])
        cp("dve", v3(G_all)[:, :, 0:4], v3(psF[:, 0:128])[:, :, 0:4], [spsF], [s_G])
        cp("dve", v3(G_all)[:, :, 4:8], v3(psF[:, 128:256])[:, :, 4:8], [spsF], [s_G])
        act(eG, G_all, AF.Exp, [s_G], [s_eG])
        ts("dve", negeG, eG, -1.0, ALU.mult, [s_eG], [s_negeG])
        tt("dve", edec, psF[:, 256:384], G_all, ALU.subtract, [spsF, s_G], [s_edec])
        act(edec, edec, AF.Exp, [s_edec], [s_edec])
        act(egl, psF[:, 256:384], AF.Exp, [spsF], [s_egl])

        for h in range(4):
            ea = Alloc(k, EXT0, EXT1)
            cin, s_cin = ea("cin", S + 4, F32)
            acc, s_acc = ea("acc", S, F32)
            vT, s_vT = ea("vT", S, BF16)
            memset("pool", cin[:, 0:2], 0.0, [s_cin])
            memset("pool", cin[:, S + 2:S + 4], 0.0, [s_cin])
            for t_i, tname in enumerate("qkv"):
                W, sW = load_chunk(l, ("dn", tname, h))
                for j in range(NT):
                    ps, sps = k.next_ps()
                    for c in range(8):
                        mm(ps[:], W[:, c, :], hT_full[:, c, j * 512:(j + 1) * 512], c == 0, c == 7, [sW, s_hTf], [sps])
                    cp("act", cin[:, 2 + j * 512:2 + (j + 1) * 512], ps[:], [sps], [s_cin])
                cw = lambda tap: ppv(l, "conv", (t_i * 4 + h) * 5 + tap)
                ts("dve", acc, cin[:, 0:S], cw(0), ALU.mult, [s_cin, s_pp], [s_acc])
                for tap in range(1, 5):
                    stt(acc, cin[:, tap:tap + S], cw(tap), acc, ALU.mult, ALU.add, [s_cin, s_pp, s_acc], [s_acc])
                if tname == "v":
                    act(vT, acc, AF.Silu, [s_acc], [s_vT])
                    continue
                act(acc, acc, AF.Silu, [s_acc], [s_acc])
                act(sq_flat[:, 0:S], acc, AF.Square, [s_acc], [s_sq])
                for j in range(NT):
                    cols = slice(j * 512, (j + 1) * 512)
                    ps, sps = k.next_ps()
                    mm(ps[:], onesb, sq_flat[:, cols], True, True, [s_cb, s_sq], [sps])
                    act(rstd_t, ps[:], AF.Sqrt, [sps, s_const], [s_rstd], bias=c_l2eps, scale=1.0)
                    P.op("dve", lambda e: e.reciprocal(out=rstd_t, in_=rstd_t), reads=[s_rstd], writes=[s_rstd])
                    if tname == "q":
                        stt(qT[:, cols], acc[:, cols], c_qs, rstd_t, ALU.mult, ALU.mult, [s_acc, s_rstd, s_const], [s_qT])
                    else:
                        tt("dve", kT[:, cols], acc[:, cols], rstd_t, ALU.mult, [s_acc, s_rstd], [s_kT])
            for g4 in range(4):
                bs = slice(g4 * 4, g4 * 4 + 4)
                ps, sps = k.next_ps()
                for bb in range(4):
                    b = g4 * 4 + bb
                    mm(ps[:, bb * 128:(bb + 1) * 128], vT[:, b * 128:(b + 1) * 128], ident, True, True, [s_vT, s_cb], [sps])
                cp("act", v_tok[:, bs, :], ps[:].rearrange("p (b c) -> p b c", c=128), [sps], [s_vtok])
                ps2, sps2 = k.next_ps()
                for bb in range(4):
                    b = g4 * 4 + bb
                    mm(ps2[:, bb * 128:(bb + 1) * 128], kT[:, b * 128:(b + 1) * 128], ident, True, True, [s_kT, s_cb], [sps2])
                for d_ in range(2):
                    col = d_ * 4 + h
                    tt("dve", kdec[d_][0][:, bs, :], ps2[:].rearrange("p (b c) -> p b c", c=128),
                       v3(edec)[:, bs, col:col + 1].to_broadcast([128, 4, 128]), ALU.mult, [sps2, s_edec], [kdec[d_][1]])
            if h == 0:
                k.dump("qT", qT, s_qT, [128, S], BF16)
                k.dump("kT", kT, s_kT, [128, S], BF16)
                k.dump("vtok", vtok_f, s_vtok, [128, S], BF16)
                k.dump("beta", beta_all, s_beta, [128, 128], F32)
                k.dump("Gall", G_all, s_G, [128, 128], F32)
            P.barrier()
            ga = Alloc(k, EXT0, EXT1)
            gU, s_gU = ga("gU", 512, F32)
            Em, s_Em = ga("Em", 512, F32)
            Dm, s_Dm = ga("Dm", 512, F32)
            eGr, s_eGr = ga("eGr", 512, F32)
            names = ["N", "NT"] + [f"N{i}" for i in range(4)] + [f"NT{i}" for i in range(4)] + \
                    ["ImN0", "ImNT0", "A2", "B2", "IpA2", "IpB2", "A4", "B4", "IpA4", "IpB4", "IpA8", "IpB8",
                     "P1", "Q1", "P2", "Q2", "X0", "Z0", "Y", "Yp", "X1", "Z1", "X2", "Z2"]
            T = {}
            for nme in names:
                T[nme] = ga(nme, 512, BF16)
            b3 = lambda t: t.rearrange("p (b c) -> p b c", c=128)
            identb4 = ident.unsqueeze(1).to_broadcast([128, 4, 128])

            def mmblk(Xn, Yn):
                X, sX = T[Xn] if isinstance(Xn, str) else Xn
                Y, sY = T[Yn] if isinstance(Yn, str) else Yn
                ps, sps = k.next_ps()
                for bb in range(4):
                    c_ = slice(bb * 128, (bb + 1) * 128)
                    mm(ps[:, c_], X[:, c_], Y[:, c_], True, True, [sX, sY], [sps])
                return ps, sps

            def ev_copy(ps, sps, dst):
                cp("act", T[dst][0], ps[:], [sps], [T[dst][1]])

            def ev_plusI(ps, sps, dst):
                tt("dve", b3(T[dst][0]), b3(ps[:]), identb4, ALU.add, [sps, s_cb], [T[dst][1]])

            def ev_sub(ps, sps, src, dst):
                tt("dve", dst[0], T[src][0] if isinstance(src, str) else src[0], ps[:], ALU.subtract,
                   [sps, T[src][1] if isinstance(src, str) else src[1]], [dst[1]])

            for d_ in range(2):
                col = d_ * 4 + h
                Umat = cfv("Uf") if d_ == 0 else cfv("Ub")
                NEG = cfv("NEGf") if d_ == 0 else cfv("NEGb")
                mN = "MU" if d_ == 0 else "ML"
                mNT = "ML" if d_ == 0 else "MU"
                for g4 in range(4):
                    bs = slice(g4 * 4, g4 * 4 + 4)
                    tcols = slice(g4 * 512, (g4 + 1) * 512)
                    tt("dve", b3(gU), Umat.unsqueeze(1).to_broadcast([128, 4, 128]),
                       v3(g_all)[:, bs, col:col + 1].to_broadcast([128, 4, 128]), ALU.mult, [s_cf, s_g], [s_gU])
                    psE, spsE = k.next_ps()
                    mm(psE[:], cfv("ones"), gU, True, False, [s_cf, s_gU], [spsE])
                    for bb in range(4):
                        c_ = slice(bb * 128, (bb + 1) * 128)
                        mm(psE[:, c_], gU[:, c_], cfv("negones"), False, bb == 3, [s_cf, s_gU], [spsE])
                    stt(b3(Em), b3(psE[:]), 0.0, NEG.unsqueeze(1).to_broadcast([128, 4, 128]), ALU.min, ALU.add, [spsE, s_cf], [s_Em])
                    act(Dm, Em, AF.Exp, [s_Em], [s_Dm])
                    psG, spsG = k.next_ps()
                    mm(psG[:], cfv("ones"), gU, True, True, [s_cf, s_gU], [spsG])
                    act(eGr, psG[:], AF.Exp, [spsG], [s_eGr])
                    tt("dve", qdTa[d_][0][:, tcols], qT[:, tcols], eGr, ALU.mult, [s_qT, s_eGr], [qdTa[d_][1][g4]])
                    kTg = (kT[:, tcols], s_kT)
                    psK, spsK = mmblk(kTg, kTg)
                    for bb in range(4):
                        c_ = slice(bb * 128, (bb + 1) * 128)
                        b = g4 * 4 + bb
                        stt(T["N"][0][:, c_], psK[:, c_], v3(beta_all)[:, b, col:col + 1], Dm[:, c_], ALU.mult, ALU.mult,
                            [spsK, s_beta, s_Dm], [T["N"][1]])
                    psQ, spsQ = mmblk(kTg, (qT[:, tcols], s_qT))
                    tt("dve", QKTa[d_][0][:, tcols], psQ[:], Dm, ALU.mult, [spsQ, s_Dm], [QKTa[d_][1][g4]])
                    psT_, spsT_ = k.next_ps()
                    for bb in range(4):
                        c_ = slice(bb * 128, (bb + 1) * 128)
                        mm(psT_[:, c_], T["N"][0][:, c_], ident, True, True, [T["N"][1], s_cb], [spsT_])
                    ev_copy(psT_, spsT_, "NT")
                    for lv in range(4):
                        tt("pool", b3(T[f"N{lv}"][0]), b3(T["N"][0]), cbv(mN, lv).unsqueeze(1).to_broadcast([128, 4, 128]), ALU.mult,
                           [T["N"][1], s_cb], [T[f"N{lv}"][1]])
                        tt("pool", b3(T[f"NT{lv}"][0]), b3(T["NT"][0]), cbv(mNT, lv).unsqueeze(1).to_broadcast([128, 4, 128]), ALU.mult,
                           [T["NT"][1], s_cb], [T[f"NT{lv}"][1]])
                    tt("pool", b3(T["ImN0"][0]), identb4, b3(T["N0"][0]), ALU.subtract, [T["N0"][1], s_cb], [T["ImN0"][1]])
                    tt("pool", b3(T["ImNT0"][0]), identb4, b3(T["NT0"][0]), ALU.subtract, [T["NT0"][1], s_cb], [T["ImNT0"][1]])
                    p_, s_ = mmblk("NT0", "N0"); ev_copy(p_, s_, "A2"); ev_plusI(p_, s_, "IpA2")
                    p_, s_ = mmblk("N0", "NT0"); ev_copy(p_, s_, "B2"); ev_plusI(p_, s_, "IpB2")
                    p_, s_ = mmblk("B2", "A2"); ev_copy(p_, s_, "A4"); ev_plusI(p_, s_, "IpA4")
                    p_, s_ = mmblk("A2", "B2"); ev_copy(p_, s_, "B4"); ev_plusI(p_, s_, "IpB4")
                    p_, s_ = mmblk("B4", "A4"); ev_plusI(p_, s_, "IpA8")
                    p_, s_ = mmblk("A4", "B4"); ev_plusI(p_, s_, "IpB8")
                    p_, s_ = mmblk("ImNT0", "IpA2"); ev_copy(p_, s_, "P1")
                    p_, s_ = mmblk("ImN0", "IpB2"); ev_copy(p_, s_, "Q1")
                    p_, s_ = mmblk("IpB4", "P1"); ev_copy(p_, s_, "P2")
                    p_, s_ = mmblk("IpA4", "Q1"); ev_copy(p_, s_, "Q2")
                    p_, s_ = mmblk("IpB8", "P2"); ev_copy(p_, s_, "X0")
                    p_, s_ = mmblk("IpA8", "Q2"); ev_copy(p_, s_, "Z0")
                    Xp, Zp = "X0", "Z0"
                    for lv in (1, 2):
                        p_, s_ = mmblk(f"NT{lv}", Xp); ev_copy(p_, s_, "Y")
                        p_, s_ = mmblk(Zp, "Y"); ev_sub(p_, s_, Xp, T[f"X{lv}"])
                        p_, s_ = mmblk(f"N{lv}", Zp); ev_copy(p_, s_, "Yp")
                        p_, s_ = mmblk(Xp, "Yp"); ev_sub(p_, s_, Zp, T[f"Z{lv}"])
                        Xp, Zp = f"X{lv}", f"Z{lv}"
                    p_, s_ = mmblk("NT3", Xp); ev_copy(p_, s_, "Y")
                    p_, s_ = mmblk(Zp, "Y")
                    ev_sub(p_, s_, Xp, (Wall[d_][0][:, tcols], Wall[d_][1][g4]))
            if h == 0:
                for g_ in range(4):
                    k.dump(f"W0_{g_}", Wall[0][0][:, g_ * 512:(g_ + 1) * 512], Wall[0][1][g_], [128, 512], BF16)
                    k.dump(f"W1_{g_}", Wall[1][0][:, g_ * 512:(g_ + 1) * 512], Wall[1][1][g_], [128, 512], BF16)
                    k.dump(f"QK0_{g_}", QKTa[0][0][:, g_ * 512:(g_ + 1) * 512], QKTa[0][1][g_], [128, 512], BF16)
            memset("pool", oacc_f, 0.0, [s_oacc])
            for d_ in range(2):
                memset("pool", St[d_][0], 0.0, [St[d_][1]])
                memset("pool", Sb[d_][0], 0.0, [Sb[d_][1]])
            for step in range(NB):
                for d_ in range(2):
                    b = step if d_ == 0 else NB - 1 - step
                    col = d_ * 4 + h
                    g4 = b // 4
                    c_ = slice(b * 128, (b + 1) * 128)
                    S_, sS = St[d_]
                    Sb_, sSb = Sb[d_]
                    R_, sR = Rt[d_]
                    V_, sV = Vn[d_]
                    ps, sps = k.next_ps()
                    mm(ps[:, 0:128], kT[:, c_], Sb_, True, True, [s_kT, sSb], [sps])
                    stt(R_, ps[:, 0:128], v3(negeG)[:, b, col:col + 1], v_tok[:, b, :], ALU.mult, ALU.add, [sps, s_negeG, s_vtok], [sR])
                    mm(ps[:, 128:256], Wall[d_][0][:, c_], R_, True, True, [Wall[d_][1][g4], sR], [sps])
                    ts("dve", V_, ps[:, 128:256], v3(beta_all)[:, b, col:col + 1], ALU.mult, [sps, s_beta], [sV])
                    mm(ps[:, 256:384], qdTa[d_][0][:, c_], Sb_, True, False, [qdTa[d_][1][g4], sSb], [sps])
                    mm(ps[:, 256:384], QKTa[d_][0][:, c_], V_, False, True, [QKTa[d_][1][g4], sV], [sps])
                    mm(ps[:, 384:512], kdec[d_][0][:, b, :], V_, True, True, [kdec[d_][1], sV], [sps])
                    tt("dve", o_acc[:, b, :], ps[:, 256:384], o_acc[:, b, :], ALU.add, [sps, s_oacc], [s_oacc])
                    stt(S_, S_, v3(egl)[:, b, col:col + 1], ps[:, 384:512], ALU.mult, ALU.add, [sS, s_egl, sps], [sS])
                    cp("act", Sb_, S_, [sS], [sSb])
            if h == 0:
                k.dump("oacc", oacc_f, s_oacc, [128, S], F32)
            P.barrier()
            da = Alloc(k, EXT0, EXT1)
            sqo, s_sqo = da("sqo", S, F32)
            rn, s_rn = da("rn", NB, F32)
            zs = [da(f"zs{i}", 512, F32) for i in range(2)]
            yat = [da(f"yat{i}", 512, BF16) for i in range(2)]
            tt("dve", sqo, oacc_f, oacc_f, ALU.mult, [s_oacc], [s_sqo])
            P.op("dve", lambda e: e.reduce_sum(out=rn, in_=sqo.rearrange("p (b c) -> p b c", c=128), axis=AX.X), reads=[s_sqo], writes=[s_rn])
            act(rn, rn, AF.Sqrt, [s_rn, s_const], [s_rn], bias=c_eps, scale=1.0 / 128)
            P.op("dve", lambda e: e.reciprocal(out=rn, in_=rn), reads=[s_rn], writes=[s_rn])
            tt("dve", o_acc, o_acc, rn.unsqueeze(2).to_broadcast([128, NB, 128]), ALU.mult, [s_oacc, s_rn], [s_oacc])
            tt("dve", o_acc, o_acc, ppv(l, "out_norm").unsqueeze(1).to_broadcast([128, NB, 128]), ALU.mult, [s_oacc, s_pp], [s_oacc])
            Wz, sWz = load_chunk(l, ("dn", "z", h))
            for g4 in range(4):
                bs = slice(g4 * 4, g4 * 4 + 4)
                ps, sps = k.next_ps()
                for bb in range(4):
                    b = g4 * 4 + bb
                    for c in range(8):
                        mm(ps[:, bb * 128:(bb + 1) * 128], hT_full[:, c, b * 128:(b + 1) * 128], Wz[:, c, :], c == 0, c == 7, [s_hTf, sWz], [sps])
                z_, sz_ = zs[g4 % 2]
                y_, sy_ = yat[g4 % 2]
                act(z_, ps[:], AF.Silu, [sps], [sz_])
                tt("dve", y_, oacc_f[:, g4 * 512:(g4 + 1) * 512], z_, ALU.mult, [s_oacc, sz_], [sy_])
                ps2, sps2 = k.next_ps()
                for bb in range(4):
                    c_ = slice(bb * 128, (bb + 1) * 128)
                    mm(ps2[:, c_], y_[:, c_], ident, True, True, [sy_, s_cb], [sps2])
                cp("act", y_a[:, h, g4 * 512:(g4 + 1) * 512], ps2[:], [sps2], [s_ya])
            P.barrier()

    def attention(l):
        aa = Alloc(k, XR, 64 * KB)
        num, s_num = aa("num", S, F32)
        den, s_den = aa("den", S, F32)
        qg, s_qg = aa("qg", S, BF16)
        kg, s_kg = aa("kg", S, BF16)
        vp_f, s_vp = aa("vp", NB * 2 * 128, BF16)
        Vp = vp_f.rearrange("p (t h c) -> p t h c", t=NB, h=2)
        cs = [aa(f"cs{i}", 1024, F32) for i in range(2)]
        qb = [aa(f"qb{i}", 512, BF16) for i in range(2)]
        t1 = [aa(f"t1{i}", 512, F32) for i in range(2)]
        t2 = [aa(f"t2{i}", 512, F32) for i in range(2)]
        Pt = [aa(f"P{i}", 256, BF16) for i in range(4)]
        Pm = [aa(f"Pm{i}", 256, BF16) for i in range(12)]
        cnt = [0, 0, 0]
        import os as _os
        DA_LEVEL = int(_os.environ.get("DA_LEVEL", "4"))
        DA_GROUPS = [int(c_) for c_ in _os.environ.get("DA_GROUPS", "012")]
        if DA_LEVEL < 4 or len(DA_GROUPS) < 3:
            memset("pool", num, 1.0, [s_num]); memset("pool", den, 1.0, [s_den])
        for hp in range(2):
            for gi, (_, dil) in enumerate(DA_CFG):
                if gi not in DA_GROUPS:
                    continue
                L = S // dil
                hview = lambda c: hT_full[:, c, :].rearrange("p (m r) -> p r m", r=dil)

                def tile_ap(c, n0, n):
                    r, m0 = divmod(n0, L)
                    if n <= L:
                        return hview(c)[:, r, m0:m0 + n]
                    return hview(c)[:, r:r + n // L, :]
                if DA_LEVEL < 1:
                    continue
                for nm, dst, sdst in (("q", qg, s_qg), ("k", kg, s_kg)):
                    W, sW = load_chunk(l, ("da", nm, gi * 2 + hp))
                    for j in range(NT):
                        cols = slice(j * 512, (j + 1) * 512)
                        ps, sps = k.next_ps()
                        for c in range(8):
                            rhs = tile_ap(c, j * 512, 512)
                            out_ = ps[:] if L >= 512 else ps[:].rearrange("p (r m) -> p r m", m=L)
                            mm(out_, W[:, c, :], rhs, c == 0, c == 7, [sW, s_hTf], [sps])
                        cnt[0] += 1
                        cst, scs = cs[cnt[0] % 2]
                        SK = _os.environ.get("DA_SKIP", "")
                        if "r" in SK:
                            memset("pool", cst, 0.5, [scs])
                        else:
                            dma(cst[:, 0:512], k.rope_d[gi, 0, :, cols], [], [scs], scs)
                            dma(cst[:, 512:1024], k.rope_d[gi, 1, :, cols], [], [scs], scs)
                        i2 = cnt[1] % 2
                        cnt[1] += 1
                        cp("act" if "c" not in SK else "dve", qb[i2][0], ps[:], [sps], [qb[i2][1]])
                        ps2, sps2 = k.next_ps()
                        mm(ps2[:], cbv("PERM") if "p" not in SK else ident, qb[i2][0], True, True, [s_cb, qb[i2][1]], [sps2])
                        sc = 0.125 if nm == "q" else 1.0
                        stt(t1[i2][0], ps[:], sc, cst[:, 0:512], ALU.mult, ALU.mult, [sps, scs], [t1[i2][1]])
                        stt(t2[i2][0], ps2[:], sc, cst[:, 512:1024], ALU.mult, ALU.mult, [sps2, scs], [t2[i2][1]])
                        tt("pool" if "a" not in SK else "dve", dst[:, cols], t1[i2][0], t2[i2][0], ALU.add, [t1[i2][1], t2[i2][1]], [sdst])
                if DA_LEVEL < 2:
                    continue
                memset("pool", vp_f, 0.0, [s_vp])
                Wv, sWv = load_chunk(l, ("da", "v", gi * 2 + hp))
                for g4 in range(4):
                    ps, sps = k.next_ps()
                    for bb in range(4):
                        t_ = g4 * 4 + bb
                        for c in range(8):
                            mm(ps[:, bb * 128:(bb + 1) * 128], tile_ap(c, t_ * 128, 128), Wv[:, c, :], c == 0, c == 7, [s_hTf, sWv], [sps])
                    pv = ps[:].rearrange("p (b c) -> p b c", c=128)
                    cp("act", Vp[:, g4 * 4:g4 * 4 + 4, 0, 0:64], pv[:, :, 0:64], [sps], [s_vp])
                    cp("dve", Vp[:, g4 * 4:g4 * 4 + 4, 1, 64:128], pv[:, :, 64:128], [sps], [s_vp])
                if DA_LEVEL < 3:
                    continue
                nkt = L // 128
                for r in range(dil):
                    ptiles = {}
                    runs = []
                    nblk = nkt + 1
                    for u0 in range(0, nblk, 4):
                        runs.append((u0, min(4, nblk - u0)))
                    for (u0, nu) in runs:
                        for kt in range(max(0, u0 - 1), min(nkt, u0 + nu)):
                            if kt in ptiles:
                                continue
                            m0 = kt * 128
                            qlo, qhi = max(0, m0 - 64), min(L, m0 + 192)
                            nq = qhi - qlo
                            boff = qlo - (m0 - 64)
                            ent = []
                            for hd in range(2):
                                pr = slice(hd * 64, (hd + 1) * 64)
                                ps, sps = k.next_ps()
                                mm(ps[:, 0:nq], kg[pr, r * L + m0:r * L + m0 + 128], qg[pr, r * L + qlo:r * L + qhi], True, True, [s_kg, s_qg], [sps])
                                p_, sp_ = Pt[cnt[2] % 4]
                                pm_, spm_ = Pm[cnt[2] % 12]
                                cnt[2] += 1
                                act(p_[:, 0:nq], ps[:, 0:nq], AF.Exp, [sps], [sp_])
                                tt("pool", pm_[:, 0:nq], p_[:, 0:nq], cbv("BAND")[:, boff:boff + nq], ALU.mult, [sp_, s_cb], [spm_])
                                ent.append((pm_, spm_, qlo))
                            ptiles[kt] = ent
                        if DA_LEVEL < 4:
                            continue
                        psn, spsn = k.next_ps()
                        psd, spsd = k.next_ps()
                        base = max(0, u0 * 128 - 64)
                        tot = 0
                        for u in range(u0, u0 + nu):
                            ulo, uhi = max(0, u * 128 - 64), min(L, u * 128 + 64)
                            oc = slice(ulo - base, uhi - base)
                            contrib = [(kt, hd) for kt in (u - 1, u) if 0 <= kt < nkt for hd in range(2)]
                            for ci, (kt, hd) in enumerate(contrib):
                                pm_, spm_, qlo = ptiles[kt][hd]
                                rhs = pm_[:, ulo - qlo:uhi - qlo]
                                tpos = r * nkt + kt
                                mm(psn[:, oc], Vp[:, tpos, hd, :], rhs, ci == 0, ci == len(contrib) - 1, [s_vp, spm_], [spsn])
                                mm(psd[:, oc], cbv("OP0") if hd == 0 else cbv("OP1"), rhs, ci == 0, ci == len(contrib) - 1, [s_cb, spm_], [spsd])
                            tot = uhi - base
                        dstn = num.rearrange("p (m r) -> p r m", r=dil)[:, r, base:base + tot]
                        dstd = den.rearrange("p (m r) -> p r m", r=dil)[:, r, base:base + tot]
                        if gi == 0:
                            cp("act", dstn, psn[:, 0:tot], [spsn], [s_num])
                            cp("dve", dstd, psd[:, 0:tot], [spsd], [s_den])
                        else:
                            tt("dve", dstn, psn[:, 0:tot], dstn, ALU.add, [spsn, s_num], [s_num])
                            tt("dve", dstd, psd[:, 0:tot], dstd, ALU.add, [spsd, s_den], [s_den])
            P.op("dve", lambda e: e.reciprocal(out=den, in_=den), reads=[s_den], writes=[s_den])
            tt("dve", y_c[:, hp, :], num, den, ALU.mult, [s_num, s_den], [s_yc])
            k.dump(f"num{hp}", num, s_num, [128, S], F32)

    def pooling(l):
        qa = Alloc(k, XR, 64 * KB)
        W_ = S + 24
        ub, s_ub = qa("ub", W_, F32)
        bufs = [qa(f"pb{i}", W_, F32) for i in range(2)]
        pT, s_pT = qa("pT", S, BF16)
        e8, s_e8 = qa("e8", 16, F32)
        for t_, s_ in [(ub, s_ub)] + bufs:
            memset("pool", t_[:, 0:16], 0.0, [s_])
            memset("pool", t_[:, S + 16:S + 24], 0.0, [s_])
        for g in range(4):
            w = (2, 4, 8, 16)[g]
            Wu, sWu = load_chunk(l, ("pool", "u", g))
            Wp, sWp = load_chunk(l, ("pool", "w", g))
            for j in range(NT):
                ps, sps = k.next_ps()
                for c in range(8):
                    mm(ps[:], Wu[:, c, :], hT_full[:, c, j * 512:(j + 1) * 512], c == 0, c == 7, [sWu, s_hTf], [sps])
                cp("act", ub[:, 16 + j * 512:16 + (j + 1) * 512], ps[:], [sps], [s_ub])
            src, ssrc = ub, s_ub
            kk_ = 1
            bi = 0
            while kk_ < w:
                dst, sdst = bufs[bi % 2]
                bi += 1
                eng = "pool" if bi % 2 == 0 else "dve"
                tt(eng, dst[:, 16:S + 24], src[:, 16:S + 24], src[:, 16 - kk_:S + 24 - kk_], ALU.add, [ssrc], [sdst])
                src, ssrc = dst, sdst
                kk_ *= 2
            sh = 16 + w // 2 - 1
            stt(pT, src[:, sh:sh + S], 1.0 / w, ub[:, 16:16 + S], ALU.mult, ALU.subtract, [ssrc, s_ub], [s_pT])
            pe_o = CF["pedge"][0] + g * 16
            tt("dve", e8[:, 0:8], src[:, sh:sh + 8], cf_t[:, pe_o:pe_o + 8], ALU.mult, [ssrc, s_cf], [s_e8])
            tt("dve", e8[:, 8:16], src[:, sh + S - 8:sh + S], cf_t[:, pe_o + 8:pe_o + 16], ALU.mult, [ssrc, s_cf], [s_e8])
            tt("dve", pT[:, 0:8], e8[:, 0:8], ub[:, 16:24], ALU.subtract, [s_e8, s_ub, s_pT], [s_pT])
            tt("dve", pT[:, S - 8:S], e8[:, 8:16], ub[:, 16 + S - 8:16 + S], ALU.subtract, [s_e8, s_ub, s_pT], [s_pT])
            k.dump(f"pT{g}", pT, s_pT, [128, S], BF16)
            for j in range(NT):
                ps, sps = k.next_ps()
                mm(ps[:], Wp[:, 0, :], pT[:, j * 512:(j + 1) * 512], True, True, [sWp, s_pT], [sps])
                act(y_b[:, g, j * 512:(j + 1) * 512], ps[:], AF.Copy, [sps, s_pp], [s_yb], scale=ppv(l, "pool_scale", g))

    def merge(l):
        for j in range(NT):
            cols = slice(j * 512, (j + 1) * 512)
            src = k.xs_d.rearrange("p (c t) -> p c t", c=8)[:, :, cols]
            dma(xT[:, :, cols], src, [s_xs[j]], [s_x[j]], s_x[j])
            for d in range(8):
                for bi, (pk, ysrc, sy, nk) in enumerate((("pa", y_a, s_ya, 4), ("pb", y_b, s_yb, 4), ("pc", y_c, s_yc, 2))):
                    Wg, sWg = load_chunk(l, ("gate", bi * 8 + d))
                    Wp, sWp = load_chunk(l, (pk, d))
                    psg, spsg = k.next_ps()
                    for c in range(8):
                        mm(psg[:], Wg[:, c, :], hT_full[:, c, cols], c == 0, c == 7, [sWg, s_hTf], [spsg])
                    psy, spsy = k.next_ps()
                    for c in range(nk):
                        mm(psy[:], Wp[:, c, :], ysrc[:, c, cols], c == 0, c == nk - 1, [sWp, sy], [spsy])
                    g_, sg_ = gate_t[bi % 2]
                    act(g_, psg[:], AF.Sigmoid, [spsg, s_pp], [sg_], bias=ppv(l, "b_gate", bi * 8 + d))
                    if bi == 0:
                        tt("dve", m32[:, d, :], psy[:], g_, ALU.mult, [spsy, sg_], [s_m32])
                    else:
                        t_, st_ = gt2_t[bi % 2]
                        tt("dve", t_, psy[:], g_, ALU.mult, [spsy, sg_], [st_])
                        if bi == 1:
                            tt("pool", m32[:, d, :], m32[:, d, :], t_, ALU.add, [s_m32, st_], [s_m32])
                        else:
                            tt("pool", mb[:, d, :], m32[:, d, :], t_, ALU.add, [s_m32, st_], [s_mb])
            for d2 in range(8):
                Wo, sWo = load_chunk(l, ("wo", d2))
                ps, sps = k.next_ps()
                for c in range(8):
                    mm(ps[:], Wo[:, c, :], mb[:, c, :], c == 0, c == 7, [sWo, s_mb], [sps])
                tt("dve", xT[:, d2, cols], ps[:], xT[:, d2, cols], ALU.add, [sps, s_x[j]], [s_x[j]])

    def mixer(l):
        for j in range(NT):
            rmsnorm_tile(j, lambda c: ppv(l, "mix_norm", c), s_pp, hT_full[:, :, j * 512:(j + 1) * 512], s_hTf)
        for j in range(NT):
            cols = slice(j * 512, (j + 1) * 512)
            dst = k.xs_d.rearrange("p (c t) -> p c t", c=8)[:, :, cols]
            dma(dst, xT[:, :, cols], [s_x[j]], [s_xs[j]], s_xs[j])
        P.barrier()
        if "m_norm" in k.stages:
            return
        if "dn" in k.stages or "mix" in k.stages:
            deltanet(l)
        else:
            memset("pool", ya_flat, 0.0, [s_ya])
        k.dump("ya", ya_flat, s_ya, [128, 4 * S], BF16)
        P.barrier()
        if "da" in k.stages or "mix" in k.stages:
            attention(l)
        else:
            memset("pool", yc_flat, 0.0, [s_yc])
        k.dump("yc", yc_flat, s_yc, [128, 2 * S], BF16)
        P.barrier()
        if "pool" in k.stages or "mix" in k.stages:
            pooling(l)
        else:
            memset("pool", yb_flat, 0.0, [s_yb])
        k.dump("yb", yb_flat, s_yb, [128, 4 * S], BF16)
        P.barrier()
        if "m_set" in k.stages:
            for j in range(NT):
                cols = slice(j * 512, (j + 1) * 512)
                src = k.xs_d.rearrange("p (c t) -> p c t", c=8)[:, :, cols]
                dma(xT[:, :, cols], src, [s_xs[j]], [s_x[j]], s_x[j])
            return
        merge(l)
        P.barrier()

    has_mix = any(s_ in stages for s_ in ("mix", "dn", "da", "pool", "mrg", "m_norm", "m_set"))
    for s in range(nseq):
        for j in range(NT):
            cols = slice(j * 512, (j + 1) * 512)
            src = k.x_in[s].rearrange("p (c t) -> p c t", c=8)[:, :, cols]
            dma(xT[:, :, cols], src, [], [s_x[j]], s_x[j])
        for l in range(nlayers):
            if "ffn1" in stages:
                ffn(l, "ffn1")
            if has_mix:
                P.barrier()
                mixer(l)
            if "ffn2" in stages:
                ffn(l, "ffn2")
        for j in range(NT):
            cols = slice(j * 512, (j + 1) * 512)
            rmsnorm_tile(j, lambda c: fin_t[:, c:c + 1], s_fin, xT[:, :, cols], s_x[j])
            dst = k.out[s].rearrange("p (c t) -> p c t", c=8)[:, :, cols]
            dma(dst, xT[:, :, cols], [s_x[j]], [], s_x[j])
        if has_mix:
            P.barrier()
    P.emit()
    return k


_CONSTS = None


def _prep_inputs(inputs):
    global _CONSTS
    inp = {n: np.asarray(v) for n, v in inputs.items()}
    wf = _host_weights(inp).reshape(NL, WTOT_PAD // WROW, WROW)
    pp, fin = _host_params(inp)
    pp2 = np.ascontiguousarray(pp.transpose(1, 0, 2).reshape(128, NL * NPP))
    if _CONSTS is None:
        _CONSTS = _host_consts()
    cf, cb, rope = _CONSTS
    return inp, dict(wf=wf, pp=pp2, fin=fin, cf=cf, cb=cb, rope=rope)


def _x_to_dev(xs):
    n = xs.shape[0]
    return np.ascontiguousarray(xs.reshape(n, S, 8, 128).transpose(0, 3, 2, 1).reshape(n, 128, 8 * S))


def _x_from_dev(o):
    n = o.shape[0]
    return np.ascontiguousarray(o.reshape(n, 128, 8, S).transpose(0, 3, 2, 1).reshape(n, S, D))


NSEQ_LAUNCH = 1


def kernel(**inputs):
    inp, shared = _prep_inputs(inputs)
    x = inp["x"]
    k = build(nseq=NSEQ_LAUNCH)
    out = np.empty((NCORES * NSEQ, S, D), np.float32)
    for r in range(NSEQ // NSEQ_LAUNCH):
        in_maps = []
        for c in range(NCORES):
            m = dict(shared)
            b0 = c * NSEQ + r * NSEQ_LAUNCH
            m["x_in"] = _x_to_dev(x[b0:b0 + NSEQ_LAUNCH])
            in_maps.append(m)
        res = run_bass_kernel_spmd(k.nc, in_maps, core_ids=list(range(NCORES)))
        for c in range(NCORES):
            b0 = c * NSEQ + r * NSEQ_LAUNCH
            out[b0:b0 + NSEQ_LAUNCH] = _x_from_dev(np.asarray(res.results[c]["out"]))
    return out
```

```python
import math
import numpy as np
import concourse.bass as bass
import concourse.mybir as mybir
from concourse.bass_utils import run_bass_kernel_spmd
from contextlib import ExitStack

F32 = mybir.dt.float32
BF16 = mybir.dt.bfloat16
U8 = mybir.dt.uint8
AF = mybir.ActivationFunctionType
ALU = mybir.AluOpType
AX = mybir.AxisListType

D = 1024
S = 2048
DFF = 2816
NL = 4
NCORES = 8
NSEQ = 4
RMS_EPS = 1e-6
L2_EPS = 1e-6
NF = DFF // 128
NT = S // 512
NB = S // 128

ENGS = ("pe", "act", "dve", "pool", "sp")
STRICT_SAME_ENGINE = True


class Slot:
    __slots__ = ("name", "last_w", "readers", "sem", "sem_cnt", "excl")

    def __init__(self, name, excl=False):
        self.name = name
        self.excl = excl
        self.last_w = None
        self.readers = {}
        self.sem = None
        self.sem_cnt = 0


class Ins:
    __slots__ = ("eng", "idx", "fn", "deps", "awaited", "count", "dma", "sem", "sem_target")

    def __init__(self, eng, idx, fn, dma):
        self.eng = eng
        self.idx = idx
        self.fn = fn
        self.deps = []
        self.awaited = False
        self.count = 0
        self.dma = dma
        self.sem = None
        self.sem_target = 0


class Prog:
    def __init__(self, nc):
        self.nc = nc
        self.streams = {e: [] for e in ENGS}
        self.es = ExitStack()
        self.dma_slots = []
        self.last_dma = {}

    def sbuf(self, name, shape, dt):
        return self.es.enter_context(self.nc.sbuf_tensor(name, list(shape), dt))

    def psum(self, name, shape, dt):
        return self.es.enter_context(self.nc.psum_tensor(name, list(shape), dt))

    def new_sem(self, name):
        return self.es.enter_context(self.nc.semaphore(name))

    def op(self, eng, fn, reads=(), writes=(), dma=False, dma_slot=None, extra_deps=()):
        st = self.streams[eng]
        ins = Ins(eng, len(st), fn, dma)
        deps = list(extra_deps)
        for s in reads:
            if s.last_w is not None:
                deps.append(s.last_w)
            if s.excl:
                deps.extend(r for kk_, r in s.readers.items() if kk_ != eng)
        for s in writes:
            if s.last_w is not None:
                deps.append(s.last_w)
            deps.extend(s.readers.values())
        seen = set()
        for d in deps:
            if d is ins or id(d) in seen:
                continue
            seen.add(id(d))
            if (not d.dma) and d.eng == eng and (eng == 'pe' or not STRICT_SAME_ENGINE):
                continue
            ins.deps.append(d)
            d.awaited = True
        for s in reads:
            key = eng if not dma else ("dma", eng, ins.idx)
            s.readers[key] = ins
        for s in writes:
            s.last_w = ins
            s.readers = {}
        if dma:
            if dma_slot.sem is None:
                dma_slot.sem = self.new_sem("d_" + dma_slot.name)
                self.dma_slots.append(dma_slot)
            dma_slot.sem_cnt += 16
            ins.sem = dma_slot.sem
            ins.sem_target = dma_slot.sem_cnt
            ins.awaited = True
            self.last_dma[id(dma_slot)] = ins
        st.append(ins)
        return ins

    def barrier(self):
        extra = list(self.last_dma.values())
        for e in ENGS:
            if e != "sp" and self.streams[e]:
                extra.append(self.streams[e][-1])
        a = self.op("sp", lambda eng: eng.nop(), extra_deps=extra)
        for e in ENGS:
            if e != "sp":
                self.op(e, lambda eng: eng.nop(), extra_deps=[a])

    def emit(self):
        nc = self.nc
        esems = {e: self.new_sem("e_" + e) for e in ENGS}
        for e in ENGS:
            c = 0
            for ins in self.streams[e]:
                if ins.awaited and not ins.dma:
                    c += 1
                ins.count = c
        final_waits = [(s.sem, s.sem_cnt) for s in self.dma_slots]
        streams = self.streams
        stats = {}

        def run(e, eng):
            known = {}
            nw = 0
            for ins in streams[e]:
                for d in ins.deps:
                    if d.dma:
                        k = ("s", id(d.sem))
                        if known.get(k, 0) >= d.sem_target:
                            continue
                        known[k] = d.sem_target
                        eng.wait_ge(d.sem, d.sem_target)
                        nw += 1
                    else:
                        if known.get(d.eng, -1) >= d.idx:
                            continue
                        known[d.eng] = d.idx
                        eng.wait_ge(esems[d.eng], d.count)
                        nw += 1
                bi = ins.fn(eng)
                if ins.dma:
                    bi.then_inc(ins.sem, 16)
                elif ins.awaited:
                    bi.then_inc(esems[e], 1)
            if e == "sp":
                for sem, v in final_waits:
                    eng.wait_ge(sem, v)
            stats[e] = (len(streams[e]), nw)

        with nc.Block() as block:
            @block.sync
            def _(eng):
                run("sp", eng)

            @block.tensor
            def _(eng):
                run("pe", eng)

            @block.scalar
            def _(eng):
                run("act", eng)

            @block.vector
            def _(eng):
                run("dve", eng)

            @block.gpsimd
            def _(eng):
                run("pool", eng)
        self.stats = stats
        self.es.close()


OFF_DN_Z = 1536
OFF_DN_BA = 2048
OFF_POOL = 2064
OFF_DA = 2576


def _chunk_table():
    t = []

    def add(key, mat, c0, nc_, K, grp):
        t.append(dict(key=key, mat=mat, c0=c0, n=nc_, K=K, grp=grp))

    for f in ("ffn1", "ffn2"):
        grp = 0 if f == "ffn1" else 2
        for i in range(NF):
            add((f, "g", i), f + "_w_gate", i * 128, 128, D, grp)
            add((f, "u", i), f + "_w_up", i * 128, 128, D, grp)
        for d in range(8):
            add((f, "d", d), f + "_w_down", d * 128, 128, DFF, grp)
        if f == "ffn1":
            for h in range(4):
                add(("dn", "q", h), "w_in", h * 128, 128, D, 1)
                add(("dn", "k", h), "w_in", 512 + h * 128, 128, D, 1)
                add(("dn", "v", h), "w_in", 1024 + h * 128, 128, D, 1)
                add(("dn", "z", h), "w_in", OFF_DN_Z + h * 128, 128, D, 1)
            add(("dn", "ba"), "w_in", OFF_DN_BA, 16, D, 1)
            for g in range(4):
                add(("pool", "u", g), "w_in", OFF_POOL + g * 128, 128, D, 1)
                add(("pool", "w", g), "pool_w", g, 128, 128, 1)
            for t3, nm in enumerate("qkv"):
                for c in range(6):
                    add(("da", nm, c), "w_in", OFF_DA + t3 * 768 + c * 128, 128, D, 1)
            for d in range(8):
                add(("pa", d), "w_proj_a", d * 128, 128, 512, 1)
                add(("pb", d), "w_proj_b", d * 128, 128, 512, 1)
                add(("pc", d), "w_proj_c", d * 128, 128, 256, 1)
                add(("wo", d), "w_out", d * 128, 128, D, 1)
            for i in range(24):
                add(("gate", i), "w_gate", i * 128, 128, D, 1)
    off = 0
    tab = {}
    grp_rng = {}
    for c in t:
        sz = 128 * (c["K"] // 128) * c["n"]
        c["off"] = off
        c["sz"] = sz
        tab[c["key"]] = c
        g = c["grp"]
        lo, hi = grp_rng.get(g, (off, off))
        grp_rng[g] = (min(lo, off), off + sz)
        off += sz
    return t, tab, off, grp_rng


CHUNKS, CHTAB, WTOT, GRP_RNG = _chunk_table()
WROW = 4096
WTOT_PAD = ((WTOT + WROW * 16 - 1) // (WROW * 16)) * (WROW * 16)

PP = {}
_o = 0
for _n, _w in (("ffn1_norm", 8), ("mix_norm", 8), ("ffn2_norm", 8), ("b_gate", 24), ("pool_scale", 4),
               ("conv", 60), ("out_norm", 128), ("a_log", 8), ("dt_bias", 8)):
    PP[_n] = (_o, _w)
    _o += _w
NPP = _o


def _host_weights(inp):
    wf = np.zeros((NL, WTOT_PAD), np.float32)
    for l in range(NL):
        for c in CHUNKS:
            m = inp[c["mat"]][l]
            if c["mat"] == "pool_w":
                blk = m[c["c0"]]
            else:
                blk = m[:, c["c0"]:c["c0"] + c["n"]]
            K = c["K"]
            arr = blk.reshape(K // 128, 128, c["n"]).transpose(1, 0, 2)
            wf[l, c["off"]:c["off"] + c["sz"]] = arr.reshape(-1)
    return wf


def _host_params(inp):
    pp = np.zeros((NL, 128, NPP), np.float32)
    for l in range(NL):
        def put(name, arr):
            o, w = PP[name]
            pp[l, :, o:o + w] = arr
        put("ffn1_norm", inp["ffn1_norm"][l].reshape(8, 128).T)
        put("mix_norm", inp["mix_norm"][l].reshape(8, 128).T)
        put("ffn2_norm", inp["ffn2_norm"][l].reshape(8, 128).T)
        put("b_gate", inp["b_gate"][l].reshape(24, 128).T)
        put("pool_scale", inp["pool_scale"][l].reshape(4, 128).T)
        cv = inp["dn_conv"][l].reshape(5, 3, 4, 128).transpose(3, 1, 2, 0).reshape(128, 60)
        put("conv", cv)
        put("out_norm", np.broadcast_to(inp["dn_out_norm"][l][None, :], (128, 128)))
        put("a_log", np.broadcast_to(inp["dn_a_log"][l].reshape(1, 8), (128, 8)))
        put("dt_bias", np.broadcast_to(inp["dn_dt_bias"][l].reshape(1, 8), (128, 8)))
    fin = np.ascontiguousarray(inp["final_norm"].reshape(8, 128).T)
    return pp, fin


DA_CFG = ((128, 1), (512, 4), (2048, 16))
CF = {}
_o = 0
for _n, _w in (("ones", 128), ("negones", 128), ("Uf", 128), ("Ub", 128), ("NEGf", 128), ("NEGb", 128), ("pedge", 64)):
    CF[_n] = (_o, _w)
    _o += _w
NCF = _o
CB = {}
_o = 0
for _n, _w in (("ident", 128), ("MU", 512), ("ML", 512), ("PERM", 128), ("BAND", 256), ("OP0", 128), ("OP1", 128), ("ones", 128)):
    CB[_n] = (_o, _w)
    _o += _w
NCB = _o


def _host_consts():
    cf = np.zeros((128, NCF), np.float32)
    cb = np.zeros((128, NCB), np.float32)
    ii = np.arange(128)
    P_, F_ = ii[:, None], ii[None, :]

    def putf(n, a):
        o, w = CF[n]
        cf[:, o:o + w] = a

    def putb(n, a):
        o, w = CB[n]
        cb[:, o:o + w] = a
    putf("ones", 1.0)
    putf("negones", -1.0)
    putf("Uf", (P_ <= F_).astype(np.float32))
    putf("Ub", (P_ >= F_).astype(np.float32))
    putf("NEGf", np.where(F_ >= P_, 0.0, -30000.0))
    putf("NEGb", np.where(F_ <= P_, 0.0, -30000.0))
    pe = np.zeros((4, 16), np.float32)
    for gi, w in enumerate((2, 4, 8, 16)):
        for t in range(8):
            lo = max(t - w // 2, 0); hi = min(t + (w - w // 2), S)
            pe[gi, t] = 1.0 / (hi - lo)
            tt = S - 8 + t
            lo = max(tt - w // 2, 0); hi = min(tt + (w - w // 2), S)
            pe[gi, 8 + t] = 1.0 / (hi - lo)
    putf("pedge", np.broadcast_to(pe.reshape(1, 64), (128, 64)))
    putb("ident", np.eye(128))
    mu = np.zeros((4, 128, 128), np.float32)
    for l in range(4):
        big = 16 << l
        m = (F_ > P_) & (F_ // big == P_ // big)
        if l > 0:
            m &= (F_ // (big // 2) != P_ // (big // 2))
        mu[l] = m
    putb("MU", mu.transpose(1, 0, 2).reshape(128, 512))
    putb("ML", mu.transpose(2, 0, 1).reshape(128, 512))
    perm = np.zeros((128, 128), np.float32)
    for m in range(128):
        hh, d = divmod(m, 64)
        perm[hh * 64 + (d + 32) % 64, m] = 1.0
    putb("PERM", perm)
    a = ii[:, None]; b = np.arange(256)[None, :]
    putb("BAND", ((a <= b) & (b <= a + 128)).astype(np.float32))
    op0 = np.zeros((128, 128), np.float32); op0[:, :64] = 1.0
    op1 = np.zeros((128, 128), np.float32); op1[:, 64:] = 1.0
    putb("OP0", op0); putb("OP1", op1)
    putb("ones", 1.0)
    rope = np.zeros((3, 2, 128, S), np.float32)
    half = 32
    inv_freq = (10000.0 ** (-np.arange(half, dtype=np.float32) / half)).astype(np.float32)
    d = np.arange(128) % 64
    fr = inv_freq[d % 32]
    sgn = np.where(d < 32, -1.0, 1.0).astype(np.float32)
    for gi, (_, dil) in enumerate(DA_CFG):
        L = S // dil
        n = np.arange(S)
        tok = (n % L) * dil + (n // L)
        ang = tok.astype(np.float32)[None, :] * fr[:, None]
        rope[gi, 0] = np.cos(ang)
        rope[gi, 1] = np.sin(ang) * sgn[:, None]
    return cf, cb, rope


class K:
    def __init__(self, nseq, nlayers, stages, debug=None):
        self.nseq = nseq
        self.nlayers = nlayers
        self.stages = stages
        self.debug = debug
        nc = bass.Bass("TRN2", target_bir_lowering=False)
        self.nc = nc
        self.P = Prog(nc)
        P = self.P
        self.x_in = nc.dram_tensor("x_in", [nseq, 128, 8 * S], F32, kind="ExternalInput").ap()
        self.out = nc.dram_tensor("out", [nseq, 128, 8 * S], F32, kind="ExternalOutput").ap()
        self.wf = nc.dram_tensor("wf", [NL, WTOT_PAD // WROW, WROW], F32, kind="ExternalInput").ap()
        self.wb = nc.dram_tensor("wb", [NL, WTOT_PAD // WROW, WROW], BF16, kind="Internal").ap()
        self.pp_d = nc.dram_tensor("pp", [128, NL * NPP], F32, kind="ExternalInput").ap()
        self.fin_d = nc.dram_tensor("fin", [128, 8], F32, kind="ExternalInput").ap()
        self.cf_d = nc.dram_tensor("cf", [128, NCF], F32, kind="ExternalInput").ap()
        self.cb_d = nc.dram_tensor("cb", [128, NCB], F32, kind="ExternalInput").ap()
        self.rope_d = nc.dram_tensor("rope", [3, 2, 128, S], F32, kind="ExternalInput").ap()
        self.xs_d = nc.dram_tensor("xs", [128, 8 * S], F32, kind="Internal").ap()
        self.wb_flat = self.wb.rearrange("l r c -> l (r c)")
        self.s_wb = {}
        self.arena_bytes = 212000
        self.arena = P.sbuf("arena", [128, self.arena_bytes], U8)
        self.psums = [(P.psum(f"ps{i}", [128, 512], F32), Slot(f"ps{i}", excl=True)) for i in range(8)]
        self.ps_i = 0

    def view(self, off, n, dt):
        sz = 4 if dt == F32 else 2
        assert off % 4 == 0 and off + n * sz <= self.arena_bytes, (off, n, sz)
        return self.arena[:, off:off + n * sz].bitcast(dt)

    def dump(self, name, ap, slot, shape, dt):
        if self.debug is None or name not in self.debug:
            return
        d = self.nc.dram_tensor("dbg_" + name, list(shape), dt, kind="ExternalOutput").ap()
        ds = Slot("dbg_" + name)
        self.P.op("sp", lambda e: e.dma_start(out=d, in_=ap), reads=[slot], dma=True, dma_slot=ds)

    def next_ps(self):
        r = self.psums[self.ps_i % 8]
        self.ps_i += 1
        return r


class Alloc:
    def __init__(self, k, start, end):
        self.k, self.o, self.end = k, start, end

    def __call__(self, name, n, dt, shape3=None):
        sz = 4 if dt == F32 else 2
        self.o = (self.o + 3) // 4 * 4
        v = self.k.view(self.o, n, dt)
        self.o += n * sz
        assert self.o <= self.end, (name, self.o, self.end)
        cache = self.k.__dict__.setdefault("slot_cache", {})
        if name not in cache:
            cache[name] = Slot(name)
        return v, cache[name]


def build(nseq=NSEQ, nlayers=NL, stages=("ffn1", "mix", "ffn2"), debug=None):
    k = K(nseq, nlayers, stages, debug)
    nc, P = k.nc, k.P
    KB = 1024

    def mm(out, lhsT, rhs, start, stop, reads, writes):
        return P.op("pe", lambda e: e.matmul(out, lhsT=lhsT, rhs=rhs, start=start, stop=stop), reads=reads, writes=writes)

    def act(out, in_, func, reads, writes, bias=None, scale=None, eng="act"):
        kw = {}
        if bias is not None:
            kw["bias"] = bias
        if scale is not None:
            kw["scale"] = scale
        return P.op("act", lambda e: e.activation(out=out, in_=in_, func=func, **kw), reads=reads, writes=writes)

    def tt(eng, out, in0, in1, op, reads, writes):
        return P.op(eng, lambda e: e.tensor_tensor(out=out, in0=in0, in1=in1, op=op), reads=reads, writes=writes)

    def stt(out, in0, scalar, in1, op0, op1, reads, writes):
        return P.op("dve", lambda e: e.scalar_tensor_tensor(out=out, in0=in0, scalar=scalar, in1=in1, op0=op0, op1=op1),
                    reads=reads, writes=writes)

    def ts(eng, out, in0, s1, op0, reads, writes, s2=None, op1=None):
        if op1 is None:
            return P.op(eng, lambda e: e.tensor_scalar(out=out, in0=in0, scalar1=s1, scalar2=None, op0=op0), reads=reads, writes=writes)
        return P.op(eng, lambda e: e.tensor_scalar(out=out, in0=in0, scalar1=s1, scalar2=s2, op0=op0, op1=op1), reads=reads, writes=writes)

    def cp(eng, out, in_, reads, writes):
        if eng == "act":
            return P.op("act", lambda e: e.activation(out=out, in_=in_, func=AF.Copy), reads=reads, writes=writes)
        return P.op(eng, lambda e: e.tensor_copy(out=out, in_=in_), reads=reads, writes=writes)

    def memset(eng, out, val, writes):
        return P.op(eng, lambda e: e.memset(out, val), writes=writes)

    def dma(out, in_, reads, writes, slot, eng="sp"):
        return P.op(eng, lambda e: e.dma_start(out=out, in_=in_), reads=reads, writes=writes, dma=True, dma_slot=slot)

    XR = 0
    pa = Alloc(k, 64 * KB, k.arena_bytes)
    pp_t, s_pp = pa("pp", NL * NPP, F32)
    fin_t, s_fin = pa("fin", 8, F32)
    cf_t, s_cf = pa("cf", NCF, F32)
    cb_t, s_cb = pa("cb", NCB, BF16)
    eps_t, s_const = pa("eps", 8, F32)
    NRING = 8
    ring = [pa(f"ring{i}", 1024, BF16) for i in range(NRING)]
    sq_flat, s_sq = pa("sq", 8 * 512, BF16)
    sq_t = sq_flat.rearrange("p (c t) -> p c t", c=8)
    rstd_t, s_rstd = pa("rstd", 512, F32)
    PH = (pa.o + 31) // 32 * 32
    xT = k.view(XR, 8 * S, F32).rearrange("p (c t) -> p c t", c=8)
    s_x = [Slot(f"x{j}") for j in range(NT)]
    s_xs = [Slot(f"xs{j}") for j in range(NT)]

    def cfv(n):
        o_, w_ = CF[n]
        return cf_t[:, o_:o_ + w_]

    def cbv(n, i=None, w=128):
        o_, w_ = CB[n]
        if i is None:
            return cb_t[:, o_:o_ + w_]
        return cb_t[:, o_ + i * w:o_ + (i + 1) * w]

    onesb = cbv("ones")
    ident = cbv("ident")

    fa = Alloc(k, PH, k.arena_bytes)
    NDR = 3
    dring = [fa(f"dring{i}", NF * 128, BF16) for i in range(NDR)]
    hT_flat, s_hT = fa("hTt", 8 * 512, BF16)
    hT_t = hT_flat.rearrange("p (c t) -> p c t", c=8)
    act_flat, _ = fa("act", NF * 512, BF16)
    act_t = act_flat.rearrange("p (c t) -> p c t", c=NF)
    s_act = [Slot(f"act{f}") for f in range(NF)]
    sg_t = [fa(f"sg{i}", 512, F32) for i in range(2)]

    ma = Alloc(k, PH, k.arena_bytes)
    hTf_flat, s_hTf = ma("hTf", 8 * S, BF16)
    hT_full = hTf_flat.rearrange("p (c t) -> p c t", c=8)
    ya_flat, s_ya = ma("ya", 4 * S, BF16)
    y_a = ya_flat.rearrange("p (c t) -> p c t", c=4)
    EXT0 = ma.o
    yb_flat, s_yb = ma("yb", 4 * S, BF16)
    y_b = yb_flat.rearrange("p (c t) -> p c t", c=4)
    yc_flat, s_yc = ma("yc", 2 * S, BF16)
    y_c = yc_flat.rearrange("p (c t) -> p c t", c=2)
    MRG0 = ma.o
    m32_flat, s_m32 = ma("m32", 8 * 512, F32)
    m32 = m32_flat.rearrange("p (c t) -> p c t", c=8)
    mb_flat, s_mb = ma("mb", 8 * 512, BF16)
    mb = mb_flat.rearrange("p (c t) -> p c t", c=8)
    gate_t = [ma(f"gate{i}", 512, F32) for i in range(2)]
    gt2_t = [ma(f"gt2{i}", 512, F32) for i in range(2)]
    EXT1 = ma.o

    ring_i = [0]
    dring_i = [0]

    ROWS_PER = 256
    nrows = WTOT_PAD // WROW
    for l in range(nlayers):
        k.s_wb[l] = Slot(f"wb{l}")
    for l in range(nlayers):
        for r0 in range(0, nrows, ROWS_PER):
            r1 = min(nrows, r0 + ROWS_PER)
            if r0 * WROW >= WTOT:
                break
            dma(k.wb[l, r0:r1, :], k.wf[l, r0:r1, :], [], [k.s_wb[l]], k.s_wb[l], eng="pool")

    def load_chunk(l, key):
        c = CHTAB[key]
        if c["K"] == DFF:
            t, s = dring[dring_i[0] % NDR]
            dring_i[0] += 1
        else:
            t, s = ring[ring_i[0] % NRING]
            ring_i[0] += 1
        n = c["sz"] // 128
        src = k.wb_flat[l, c["off"]:c["off"] + c["sz"]].rearrange("(p f) -> p f", p=128)
        dma(t[:, 0:n], src, [k.s_wb[l]], [s], s)
        return t[:, 0:n].rearrange("p (k c) -> p k c", c=c["n"]), s

    dma(pp_t, k.pp_d, [], [s_pp], s_pp)
    dma(fin_t, k.fin_d, [], [s_fin], s_fin)
    dma(cf_t, k.cf_d, [], [s_cf], s_cf)
    dma(cb_t, k.cb_d, [], [s_cb], s_cb, eng="pool")
    for i, v in enumerate((RMS_EPS, L2_EPS, 0.5, 1.0, -1.0, 128.0 ** -0.5)):
        memset("pool", eps_t[:, i:i + 1], v, [s_const])
    c_eps, c_l2eps, c_half, c_one, c_neg1, c_qs = [eps_t[:, i:i + 1] for i in range(6)]

    def ppv(l, name, j=None, w=1):
        o_, w_ = PP[name]
        base = l * NPP + o_
        if j is None:
            return pp_t[:, base:base + w_]
        return pp_t[:, base + j:base + j + w]

    def rmsnorm_tile(j, gain_fn, gain_slot, out_ap, out_slot):
        cols = slice(j * 512, (j + 1) * 512)
        act(sq_t, xT[:, :, cols], AF.Square, [s_x[j]], [s_sq])
        ps, sps = k.next_ps()
        for c in range(8):
            mm(ps[:], onesb, sq_t[:, c, :], c == 0, c == 7, [s_cb, s_sq], [sps])
        act(rstd_t, ps[:], AF.Sqrt, [sps, s_const], [s_rstd], bias=c_eps, scale=1.0 / D)
        P.op("dve", lambda e: e.reciprocal(out=rstd_t, in_=rstd_t), reads=[s_rstd], writes=[s_rstd])
        for c in range(8):
            stt(out_ap[:, c, :], xT[:, c, cols], gain_fn(c), rstd_t, ALU.mult, ALU.mult, [s_x[j], gain_slot, s_rstd], [out_slot])

    def ffn(l, which):
        nm = which + "_norm"
        for j in range(NT):
            cols = slice(j * 512, (j + 1) * 512)
            rmsnorm_tile(j, lambda c: ppv(l, nm, c), s_pp, hT_t, s_hT)
            for f in range(NF):
                wg, swg = load_chunk(l, (which, "g", f))
                wu, swu = load_chunk(l, (which, "u", f))
                pg, spg = k.next_ps()
                pu, spu = k.next_ps()
                for c in range(8):
                    mm(pg[:], wg[:, c, :], hT_t[:, c, :], c == 0, c == 7, [swg, s_hT], [spg])
                for c in range(8):
                    mm(pu[:], wu[:, c, :], hT_t[:, c, :], c == 0, c == 7, [swu, s_hT], [spu])
                sg, ssg = sg_t[f % 2]
                act(sg, pg[:], AF.Silu, [spg], [ssg])
                tt("dve", act_t[:, f, :], pu[:], sg, ALU.mult, [spu, ssg], [s_act[f]])
            for d in range(8):
                wd, swd = load_chunk(l, (which, "d", d))
                pd, spd = k.next_ps()
                for f in range(NF):
                    mm(pd[:], wd[:, f, :], act_t[:, f, :], f == 0, f == NF - 1, [swd, s_act[f]], [spd])
                stt(xT[:, d, cols], pd[:], c_half, xT[:, d, cols], ALU.mult, ALU.add, [spd, s_x[j], s_const], [s_x[j]])

    def deltanet(l):
        xa = Alloc(k, XR, 64 * KB)
        beta_all, s_beta = xa("beta", 128, F32)
        g_all, s_g = xa("g", 128, F32)
        G_all, s_G = xa("G", 128, F32)
        eG, s_eG = xa("eG", 128, F32)
        negeG, s_negeG = xa("negeG", 128, F32)
        edec, s_edec = xa("edec", 128, F32)
        egl, s_egl = xa("egl", 128, F32)
        t_a, s_ta = xa("ta", 128, F32)
        negA, s_negA = xa("negA", 8, F32)
        v3 = lambda t: t.rearrange("p (b c) -> p b c", c=8)
        qT, s_qT = xa("qT", S, BF16)
        kT, s_kT = xa("kT", S, BF16)
        vtok_f, s_vtok = xa("vtok", S, BF16)
        v_tok = vtok_f.rearrange("p (b c) -> p b c", c=128)
        kdec = []
        for d_ in range(2):
            t_, s_ = xa(f"kdec{d_}", S, BF16)
            kdec.append((t_.rearrange("p (b c) -> p b c", c=128), s_))
        oacc_f, s_oacc = xa("oacc", S, F32)
        o_acc = oacc_f.rearrange("p (b c) -> p b c", c=128)
        Wall, QKTa, qdTa = [], [], []
        for d_ in range(2):
            for lst, nm in ((Wall, "W"), (QKTa, "QKT"), (qdTa, "qdT")):
                t_, s_ = xa(f"{nm}{d_}", S, BF16)
                lst.append((t_, [Slot(f"{nm}{d_}_{g}") for g in range(4)]))
        St = [xa(f"S{d_}", 128, F32) for d_ in range(2)]
        Sb = [xa(f"Sb{d_}", 128, BF16) for d_ in range(2)]
        Rt = [xa(f"R{d_}", 128, BF16) for d_ in range(2)]
        Vn = [xa(f"Vn{d_}", 128, BF16) for d_ in range(2)]

        ba, sba = load_chunk(l, ("dn", "ba"))
        ps, sps = k.next_ps()
        psv = ps[:, 0:256].rearrange("p (b c) -> p b c", c=16)
        for b in range(NB):
            for c in range(8):
                mm(ps[:, b * 16:(b + 1) * 16], hT_full[:, c, b * 128:(b + 1) * 128], ba[:, c, :], c == 0, c == 7, [s_hTf, sba], [sps])
        act(v3(beta_all), psv[:, :, 0:8], AF.Sigmoid, [sps], [s_beta])
        dtb = ppv(l, "dt_bias").unsqueeze(1).to_broadcast([128, NB, 8])
        tt("dve", v3(t_a), psv[:, :, 8:16], dtb, ALU.add, [sps, s_pp], [s_ta])
        act(t_a, t_a, AF.Exp, [s_ta], [s_ta])
        act(t_a, t_a, AF.Ln, [s_ta, s_const], [s_ta], bias=c_one)
        act(negA, ppv(l, "a_log"), AF.Exp, [s_pp], [s_negA])
        ts("dve", negA, negA, -1.0, ALU.mult, [s_negA], [s_negA])
        tt("dve", v3(g_all), v3(t_a), negA.unsqueeze(1).to_broadcast([128, NB, 8]), ALU.mult, [s_ta, s_negA], [s_g])
        psF, spsF = k.next_ps()
        mm(psF[:, 0:128], cfv("Uf"), g_all, True, True, [s_cf, s_g], [spsF])
        mm(psF[:, 128:256], cfv("Ub"), g_all, True, True, [s_cf, s_g], [spsF])
        mm(psF[:, 256:384], cfv("ones"), g_all, True, True, [s_cf, s_g], [spsF])
        cp("dve", v3(G_all)[:, :, 0:4], v3(psF[:, 0:128])[:, :, 0:4], [spsF], [s_G])
        cp("dve", v3(G_all)[:, :, 4:8], v3(psF[:, 128:256])[:, :, 4:8], [spsF], [s_G])
        act(eG, G_all, AF.Exp, [s_G], [s_eG])
        ts("dve", negeG, eG, -1.0, ALU.mult, [s_eG], [s_negeG])
        tt("dve", edec, psF[:, 256:384], G_all, ALU.subtract, [spsF, s_G], [s_edec])
        act(edec, edec, AF.Exp, [s_edec], [s_edec])
        act(egl, psF[:, 256:384], AF.Exp, [spsF], [s_egl])

        for h in range(4):
            ea = Alloc(k, EXT0, EXT1)
            cin, s_cin = ea("cin", S + 4, F32)
            acc, s_acc = ea("acc", S, F32)
            vT, s_vT = ea("vT", S, BF16)
            memset("pool", cin[:, 0:2], 0.0, [s_cin])
            memset("pool", cin[:, S + 2:S + 4], 0.0, [s_cin])
            for t_i, tname in enumerate("qkv"):
                W, sW = load_chunk(l, ("dn", tname, h))
                for j in range(NT):
                    ps, sps = k.next_ps()
                    for c in range(8):
                        mm(ps[:], W[:, c, :], hT_full[:, c, j * 512:(j + 1) * 512], c == 0, c == 7, [sW, s_hTf], [sps])
                    cp("act", cin[:, 2 + j * 512:2 + (j + 1) * 512], ps[:], [sps], [s_cin])
                cw = lambda tap: ppv(l, "conv", (t_i * 4 + h) * 5 + tap)
                ts("dve", acc, cin[:, 0:S], cw(0), ALU.mult, [s_cin, s_pp], [s_acc])
                for tap in range(1, 5):
                    stt(acc, cin[:, tap:tap + S], cw(tap), acc, ALU.mult, ALU.add, [s_cin, s_pp, s_acc], [s_acc])
                if tname == "v":
                    act(vT, acc, AF.Silu, [s_acc], [s_vT])
                    continue
                act(acc, acc, AF.Silu, [s_acc], [s_acc])
                act(sq_flat[:, 0:S], acc, AF.Square, [s_acc], [s_sq])
                for j in range(NT):
                    cols = slice(j * 512, (j + 1) * 512)
                    ps, sps = k.next_ps()
                    mm(ps[:], onesb, sq_flat[:, cols], True, True, [s_cb, s_sq], [sps])
                    act(rstd_t, ps[:], AF.Sqrt, [sps, s_const], [s_rstd], bias=c_l2eps, scale=1.0)
                    P.op("dve", lambda e: e.reciprocal(out=rstd_t, in_=rstd_t), reads=[s_rstd], writes=[s_rstd])
                    if tname == "q":
                        stt(qT[:, cols], acc[:, cols], c_qs, rstd_t, ALU.mult, ALU.mult, [s_acc, s_rstd, s_const], [s_qT])
                    else:
                        tt("dve", kT[:, cols], acc[:, cols], rstd_t, ALU.mult, [s_acc, s_rstd], [s_kT])
            for g4 in range(4):
                bs = slice(g4 * 4, g4 * 4 + 4)
                ps, sps = k.next_ps()
                for bb in range(4):
                    b = g4 * 4 + bb
                    mm(ps[:, bb * 128:(bb + 1) * 128], vT[:, b * 128:(b + 1) * 128], ident, True, True, [s_vT, s_cb], [sps])
                cp("act", v_tok[:, bs, :], ps[:].rearrange("p (b c) -> p b c", c=128), [sps], [s_vtok])
                ps2, sps2 = k.next_ps()
                for bb in range(4):
                    b = g4 * 4 + bb
                    mm(ps2[:, bb * 128:(bb + 1) * 128], kT[:, b * 128:(b + 1) * 128], ident, True, True, [s_kT, s_cb], [sps2])
                for d_ in range(2):
                    col = d_ * 4 + h
                    tt("dve", kdec[d_][0][:, bs, :], ps2[:].rearrange("p (b c) -> p b c", c=128),
                       v3(edec)[:, bs, col:col + 1].to_broadcast([128, 4, 128]), ALU.mult, [sps2, s_edec], [kdec[d_][1]])
            if h == 0:
                k.dump("qT", qT, s_qT, [128, S], BF16)
                k.dump("kT", kT, s_kT, [128, S], BF16)
                k.dump("vtok", vtok_f, s_vtok, [128, S], BF16)
                k.dump("beta", beta_all, s_beta, [128, 128], F32)
                k.dump("Gall", G_all, s_G, [128, 128], F32)
            P.barrier()
            ga = Alloc(k, EXT0, EXT1)
            gU, s_gU = ga("gU", 512, F32)
            Em, s_Em = ga("Em", 512, F32)
            Dm, s_Dm = ga("Dm", 512, F32)
            eGr, s_eGr = ga("eGr", 512, F32)
            names = ["N", "NT"] + [f"N{i}" for i in range(4)] + [f"NT{i}" for i in range(4)] + \
                    ["ImN0", "ImNT0", "A2", "B2", "IpA2", "IpB2", "A4", "B4", "IpA4", "IpB4", "IpA8", "IpB8",
                     "P1", "Q1", "P2", "Q2", "X0", "Z0", "Y", "Yp", "X1", "Z1", "X2", "Z2"]
            T = {}
            for nme in names:
                T[nme] = ga(nme, 512, BF16)
            b3 = lambda t: t.rearrange("p (b c) -> p b c", c=128)
            identb4 = ident.unsqueeze(1).to_broadcast([128, 4, 128])

            def mmblk(Xn, Yn):
                X, sX = T[Xn] if isinstance(Xn, str) else Xn
                Y, sY = T[Yn] if isinstance(Yn, str) else Yn
                ps, sps = k.next_ps()
                for bb in range(4):
                    c_ = slice(bb * 128, (bb + 1) * 128)
                    mm(ps[:, c_], X[:, c_], Y[:, c_], True, True, [sX, sY], [sps])
                return ps, sps

            def ev_copy(ps, sps, dst):
                cp("act", T[dst][0], ps[:], [sps], [T[dst][1]])

            def ev_plusI(ps, sps, dst):
                tt("dve", b3(T[dst][0]), b3(ps[:]), identb4, ALU.add, [sps, s_cb], [T[dst][1]])

            def ev_sub(ps, sps, src, dst):
                tt("dve", dst[0], T[src][0] if isinstance(src, str) else src[0], ps[:], ALU.subtract,
                   [sps, T[src][1] if isinstance(src, str) else src[1]], [dst[1]])

            for d_ in range(2):
                col = d_ * 4 + h
                Umat = cfv("Uf") if d_ == 0 else cfv("Ub")
                NEG = cfv("NEGf") if d_ == 0 else cfv("NEGb")
                mN = "MU" if d_ == 0 else "ML"
                mNT = "ML" if d_ == 0 else "MU"
                for g4 in range(4):
                    bs = slice(g4 * 4, g4 * 4 + 4)
                    tcols = slice(g4 * 512, (g4 + 1) * 512)
                    tt("dve", b3(gU), Umat.unsqueeze(1).to_broadcast([128, 4, 128]),
                       v3(g_all)[:, bs, col:col + 1].to_broadcast([128, 4, 128]), ALU.mult, [s_cf, s_g], [s_gU])
                    psE, spsE = k.next_ps()
                    mm(psE[:], cfv("ones"), gU, True, False, [s_cf, s_gU], [spsE])
                    for bb in range(4):
                        c_ = slice(bb * 128, (bb + 1) * 128)
                        mm(psE[:, c_], gU[:, c_], cfv("negones"), False, bb == 3, [s_cf, s_gU], [spsE])
                    stt(b3(Em), b3(psE[:]), 0.0, NEG.unsqueeze(1).to_broadcast([128, 4, 128]), ALU.min, ALU.add, [spsE, s_cf], [s_Em])
                    act(Dm, Em, AF.Exp, [s_Em], [s_Dm])
                    psG, spsG = k.next_ps()
                    mm(psG[:], cfv("ones"), gU, True, True, [s_cf, s_gU], [spsG])
                    act(eGr, psG[:], AF.Exp, [spsG], [s_eGr])
                    tt("dve", qdTa[d_][0][:, tcols], qT[:, tcols], eGr, ALU.mult, [s_qT, s_eGr], [qdTa[d_][1][g4]])
                    kTg = (kT[:, tcols], s_kT)
                    psK, spsK = mmblk(kTg, kTg)
                    for bb in range(4):
                        c_ = slice(bb * 128, (bb + 1) * 128)
                        b = g4 * 4 + bb
                        stt(T["N"][0][:, c_], psK[:, c_], v3(beta_all)[:, b, col:col + 1], Dm[:, c_], ALU.mult, ALU.mult,
                            [spsK, s_beta, s_Dm], [T["N"][1]])
                    psQ, spsQ = mmblk(kTg, (qT[:, tcols], s_qT))
                    tt("dve", QKTa[d_][0][:, tcols], psQ[:], Dm, ALU.mult, [spsQ, s_Dm], [QKTa[d_][1][g4]])
                    psT_, spsT_ = k.next_ps()
                    for bb in range(4):
                        c_ = slice(bb * 128, (bb + 1) * 128)
                        mm(psT_[:, c_], T["N"][0][:, c_], ident, True, True, [T["N"][1], s_cb], [spsT_])
                    ev_copy(psT_, spsT_, "NT")
                    for lv in range(4):
                        tt("pool", b3(T[f"N{lv}"][0]), b3(T["N"][0]), cbv(mN, lv).unsqueeze(1).to_broadcast([128, 4, 128]), ALU.mult,
                           [T["N"][1], s_cb], [T[f"N{lv}"][1]])
                        tt("pool", b3(T[f"NT{lv}"][0]), b3(T["NT"][0]), cbv(mNT, lv).unsqueeze(1).to_broadcast([128, 4, 128]), ALU.mult,
                           [T["NT"][1], s_cb], [T[f"NT{lv}"][1]])
                    tt("pool", b3(T["ImN0"][0]), identb4, b3(T["N0"][0]), ALU.subtract, [T["N0"][1], s_cb], [T["ImN0"][1]])
                    tt("pool", b3(T["ImNT0"][0]), identb4, b3(T["NT0"][0]), ALU.subtract, [T["NT0"][1], s_cb], [T["ImNT0"][1]])
                    p_, s_ = mmblk("NT0", "N0"); ev_copy(p_, s_, "A2"); ev_plusI(p_, s_, "IpA2")
                    p_, s_ = mmblk("N0", "NT0"); ev_copy(p_, s_, "B2"); ev_plusI(p_, s_, "IpB2")
                    p_, s_ = mmblk("B2", "A2"); ev_copy(p_, s_, "A4"); ev_plusI(p_, s_, "IpA4")
                    p_, s_ = mmblk("A2", "B2"); ev_copy(p_, s_, "B4"); ev_plusI(p_, s_, "IpB4")
                    p_, s_ = mmblk("B4", "A4"); ev_plusI(p_, s_, "IpA8")
                    p_, s_ = mmblk("A4", "B4"); ev_plusI(p_, s_, "IpB8")
                    p_, s_ = mmblk("ImNT0", "IpA2"); ev_copy(p_, s_, "P1")
                    p_, s_ = mmblk("ImN0", "IpB2"); ev_copy(p_, s_, "Q1")
                    p_, s_ = mmblk("IpB4", "P1"); ev_copy(p_, s_, "P2")
                    p_, s_ = mmblk("IpA4", "Q1"); ev_copy(p_, s_, "Q2")
                    p_, s_ = mmblk("IpB8", "P2"); ev_copy(p_, s_, "X0")
                    p_, s_ = mmblk("IpA8", "Q2"); ev_copy(p_, s_, "Z0")
                    Xp, Zp = "X0", "Z0"
                    for lv in (1, 2):
                        p_, s_ = mmblk(f"NT{lv}", Xp); ev_copy(p_, s_, "Y")
                        p_, s_ = mmblk(Zp, "Y"); ev_sub(p_, s_, Xp, T[f"X{lv}"])
                        p_, s_ = mmblk(f"N{lv}", Zp); ev_copy(p_, s_, "Yp")
                        p_, s_ = mmblk(Xp, "Yp"); ev_sub(p_, s_, Zp, T[f"Z{lv}"])
                        Xp, Zp = f"X{lv}", f"Z{lv}"
                    p_, s_ = mmblk("NT3", Xp); ev_copy(p_, s_, "Y")
                    p_, s_ = mmblk(Zp, "Y")
                    ev_sub(p_, s_, Xp, (Wall[d_][0][:, tcols], Wall[d_][1][g4]))
            if h == 0:
                for g_ in range(4):
                    k.dump(f"W0_{g_}", Wall[0][0][:, g_ * 512:(g_ + 1) * 512], Wall[0][1][g_], [128, 512], BF16)
                    k.dump(f"W1_{g_}", Wall[1][0][:, g_ * 512:(g_ + 1) * 512], Wall[1][1][g_], [128, 512], BF16)
                    k.dump(f"QK0_{g_}", QKTa[0][0][:, g_ * 512:(g_ + 1) * 512], QKTa[0][1][g_], [128, 512], BF16)
            memset("pool", oacc_f, 0.0, [s_oacc])
            for d_ in range(2):
                memset("pool", St[d_][0], 0.0, [St[d_][1]])
                memset("pool", Sb[d_][0], 0.0, [Sb[d_][1]])
            for step in range(NB):
                for d_ in range(2):
                    b = step if d_ == 0 else NB - 1 - step
                    col = d_ * 4 + h
                    g4 = b // 4
                    c_ = slice(b * 128, (b + 1) * 128)
                    S_, sS = St[d_]
                    Sb_, sSb = Sb[d_]
                    R_, sR = Rt[d_]
                    V_, sV = Vn[d_]
                    ps, sps = k.next_ps()
                    mm(ps[:, 0:128], kT[:, c_], Sb_, True, True, [s_kT, sSb], [sps])
                    stt(R_, ps[:, 0:128], v3(negeG)[:, b, col:col + 1], v_tok[:, b, :], ALU.mult, ALU.add, [sps, s_negeG, s_vtok], [sR])
                    mm(ps[:, 128:256], Wall[d_][0][:, c_], R_, True, True, [Wall[d_][1][g4], sR], [sps])
                    ts("dve", V_, ps[:, 128:256], v3(beta_all)[:, b, col:col + 1], ALU.mult, [sps, s_beta], [sV])
                    mm(ps[:, 256:384], qdTa[d_][0][:, c_], Sb_, True, False, [qdTa[d_][1][g4], sSb], [sps])
                    mm(ps[:, 256:384], QKTa[d_][0][:, c_], V_, False, True, [QKTa[d_][1][g4], sV], [sps])
                    mm(ps[:, 384:512], kdec[d_][0][:, b, :], V_, True, True, [kdec[d_][1], sV], [sps])
                    tt("dve", o_acc[:, b, :], ps[:, 256:384], o_acc[:, b, :], ALU.add, [sps, s_oacc], [s_oacc])
                    stt(S_, S_, v3(egl)[:, b, col:col + 1], ps[:, 384:512], ALU.mult, ALU.add, [sS, s_egl, sps], [sS])
                    cp("act", Sb_, S_, [sS], [sSb])
            if h == 0:
                k.dump("oacc", oacc_f, s_oacc, [128, S], F32)
            P.barrier()
            da = Alloc(k, EXT0, EXT1)
            sqo, s_sqo = da("sqo", S, F32)
            rn, s_rn = da("rn", NB, F32)
            zs = [da(f"zs{i}", 512, F32) for i in range(2)]
            yat = [da(f"yat{i}", 512, BF16) for i in range(2)]
            tt("dve", sqo, oacc_f, oacc_f, ALU.mult, [s_oacc], [s_sqo])
            P.op("dve", lambda e: e.reduce_sum(out=rn, in_=sqo.rearrange("p (b c) -> p b c", c=128), axis=AX.X), reads=[s_sqo], writes=[s_rn])
            act(rn, rn, AF.Sqrt, [s_rn, s_const], [s_rn], bias=c_eps, scale=1.0 / 128)
            P.op("dve", lambda e: e.reciprocal(out=rn, in_=rn), reads=[s_rn], writes=[s_rn])
            tt("dve", o_acc, o_acc, rn.unsqueeze(2).to_broadcast([128, NB, 128]), ALU.mult, [s_oacc, s_rn], [s_oacc])
            tt("dve", o_acc, o_acc, ppv(l, "out_norm").unsqueeze(1).to_broadcast([128, NB, 128]), ALU.mult, [s_oacc, s_pp], [s_oacc])
            Wz, sWz = load_chunk(l, ("dn", "z", h))
            for g4 in range(4):
                bs = slice(g4 * 4, g4 * 4 + 4)
                ps, sps = k.next_ps()
                for bb in range(4):
                    b = g4 * 4 + bb
                    for c in range(8):
                        mm(ps[:, bb * 128:(bb + 1) * 128], hT_full[:, c, b * 128:(b + 1) * 128], Wz[:, c, :], c == 0, c == 7, [s_hTf, sWz], [sps])
                z_, sz_ = zs[g4 % 2]
                y_, sy_ = yat[g4 % 2]
                act(z_, ps[:], AF.Silu, [sps], [sz_])
                tt("dve", y_, oacc_f[:, g4 * 512:(g4 + 1) * 512], z_, ALU.mult, [s_oacc, sz_], [sy_])
                ps2, sps2 = k.next_ps()
                for bb in range(4):
                    c_ = slice(bb * 128, (bb + 1) * 128)
                    mm(ps2[:, c_], y_[:, c_], ident, True, True, [sy_, s_cb], [sps2])
                cp("act", y_a[:, h, g4 * 512:(g4 + 1) * 512], ps2[:], [sps2], [s_ya])
            P.barrier()

    def attention(l):
        aa = Alloc(k, XR, 64 * KB)
        num, s_num = aa("num", S, F32)
        den, s_den = aa("den", S, F32)
        qg, s_qg = aa("qg", S, BF16)
        kg, s_kg = aa("kg", S, BF16)
        vp_f, s_vp = aa("vp", NB * 2 * 128, BF16)
        Vp = vp_f.rearrange("p (t h c) -> p t h c", t=NB, h=2)
        cs = [aa(f"cs{i}", 1024, F32) for i in range(2)]
        qb = [aa(f"qb{i}", 512, BF16) for i in range(2)]
        t1 = [aa(f"t1{i}", 512, F32) for i in range(2)]
        t2 = [aa(f"t2{i}", 512, F32) for i in range(2)]
        Pt = [aa(f"P{i}", 256, BF16) for i in range(4)]
        Pm = [aa(f"Pm{i}", 256, BF16) for i in range(12)]
        cnt = [0, 0, 0]
        import os as _os
        DA_LEVEL = int(_os.environ.get("DA_LEVEL", "4"))
        DA_GROUPS = [int(c_) for c_ in _os.environ.get("DA_GROUPS", "012")]
        if DA_LEVEL < 4 or len(DA_GROUPS) < 3:
            memset("pool", num, 1.0, [s_num]); memset("pool", den, 1.0, [s_den])
        for hp in range(2):
            for gi, (_, dil) in enumerate(DA_CFG):
                if gi not in DA_GROUPS:
                    continue
                L = S // dil
                hview = lambda c: hT_full[:, c, :].rearrange("p (m r) -> p r m", r=dil)

                def tile_ap(c, n0, n):
                    r, m0 = divmod(n0, L)
                    if n <= L:
                        return hview(c)[:, r, m0:m0 + n]
                    return hview(c)[:, r:r + n // L, :]
                if DA_LEVEL < 1:
                    continue
                for nm, dst, sdst in (("q", qg, s_qg), ("k", kg, s_kg)):
                    W, sW = load_chunk(l, ("da", nm, gi * 2 + hp))
                    for j in range(NT):
                        cols = slice(j * 512, (j + 1) * 512)
                        ps, sps = k.next_ps()
                        for c in range(8):
                            rhs = tile_ap(c, j * 512, 512)
                            out_ = ps[:] if L >= 512 else ps[:].rearrange("p (r m) -> p r m", m=L)
                            mm(out_, W[:, c, :], rhs, c == 0, c == 7, [sW, s_hTf], [sps])
                        cnt[0] += 1
                        cst, scs = cs[cnt[0] % 2]
                        SK = _os.environ.get("DA_SKIP", "")
                        if "r" in SK:
                            memset("pool", cst, 0.5, [scs])
                        else:
                            dma(cst[:, 0:512], k.rope_d[gi, 0, :, cols], [], [scs], scs)
                            dma(cst[:, 512:1024], k.rope_d[gi, 1, :, cols], [], [scs], scs)
                        i2 = cnt[1] % 2
                        cnt[1] += 1
                        cp("act" if "c" not in SK else "dve", qb[i2][0], ps[:], [sps], [qb[i2][1]])
                        ps2, sps2 = k.next_ps()
                        mm(ps2[:], cbv("PERM") if "p" not in SK else ident, qb[i2][0], True, True, [s_cb, qb[i2][1]], [sps2])
                        sc = 0.125 if nm == "q" else 1.0
                        stt(t1[i2][0], ps[:], sc, cst[:, 0:512], ALU.mult, ALU.mult, [sps, scs], [t1[i2][1]])
                        stt(t2[i2][0], ps2[:], sc, cst[:, 512:1024], ALU.mult, ALU.mult, [sps2, scs], [t2[i2][1]])
                        tt("pool" if "a" not in SK else "dve", dst[:, cols], t1[i2][0], t2[i2][0], ALU.add, [t1[i2][1], t2[i2][1]], [sdst])
                if DA_LEVEL < 2:
                    continue
                memset("pool", vp_f, 0.0, [s_vp])
                Wv, sWv = load_chunk(l, ("da", "v", gi * 2 + hp))
                for g4 in range(4):
                    ps, sps = k.next_ps()
                    for bb in range(4):
                        t_ = g4 * 4 + bb
                        for c in range(8):
                            mm(ps[:, bb * 128:(bb + 1) * 128], tile_ap(c, t_ * 128, 128), Wv[:, c, :], c == 0, c == 7, [s_hTf, sWv], [sps])
                    pv = ps[:].rearrange("p (b c) -> p b c", c=128)
                    cp("act", Vp[:, g4 * 4:g4 * 4 + 4, 0, 0:64], pv[:, :, 0:64], [sps], [s_vp])
                    cp("dve", Vp[:, g4 * 4:g4 * 4 + 4, 1, 64:128], pv[:, :, 64:128], [sps], [s_vp])
                if DA_LEVEL < 3:
                    continue
                nkt = L // 128
                for r in range(dil):
                    ptiles = {}
                    runs = []
                    nblk = nkt + 1
                    for u0 in range(0, nblk, 4):
                        runs.append((u0, min(4, nblk - u0)))
                    for (u0, nu) in runs:
                        for kt in range(max(0, u0 - 1), min(nkt, u0 + nu)):
                            if kt in ptiles:
                                continue
                            m0 = kt * 128
                            qlo, qhi = max(0, m0 - 64), min(L, m0 + 192)
                            nq = qhi - qlo
                            boff = qlo - (m0 - 64)
                            ent = []
                            for hd in range(2):
                                pr = slice(hd * 64, (hd + 1) * 64)
                                ps, sps = k.next_ps()
                                mm(ps[:, 0:nq], kg[pr, r * L + m0:r * L + m0 + 128], qg[pr, r * L + qlo:r * L + qhi], True, True, [s_kg, s_qg], [sps])
                                p_, sp_ = Pt[cnt[2] % 4]
                                pm_, spm_ = Pm[cnt[2] % 12]
                                cnt[2] += 1
                                act(p_[:, 0:nq], ps[:, 0:nq], AF.Exp, [sps], [sp_])
                                tt("pool", pm_[:, 0:nq], p_[:, 0:nq], cbv("BAND")[:, boff:boff + nq], ALU.mult, [sp_, s_cb], [spm_])
                                ent.append((pm_, spm_, qlo))
                            ptiles[kt] = ent
                        if DA_LEVEL < 4:
                            continue
                        psn, spsn = k.next_ps()
                        psd, spsd = k.next_ps()
                        base = max(0, u0 * 128 - 64)
                        tot = 0
                        for u in range(u0, u0 + nu):
                            ulo, uhi = max(0, u * 128 - 64), min(L, u * 128 + 64)
                            oc = slice(ulo - base, uhi - base)
                            contrib = [(kt, hd) for kt in (u - 1, u) if 0 <= kt < nkt for hd in range(2)]
                            for ci, (kt, hd) in enumerate(contrib):
                                pm_, spm_, qlo = ptiles[kt][hd]
                                rhs = pm_[:, ulo - qlo:uhi - qlo]
                                tpos = r * nkt + kt
                                mm(psn[:, oc], Vp[:, tpos, hd, :], rhs, ci == 0, ci == len(contrib) - 1, [s_vp, spm_], [spsn])
                                mm(psd[:, oc], cbv("OP0") if hd == 0 else cbv("OP1"), rhs, ci == 0, ci == len(contrib) - 1, [s_cb, spm_], [spsd])
                            tot = uhi - base
                        dstn = num.rearrange("p (m r) -> p r m", r=dil)[:, r, base:base + tot]
                        dstd = den.rearrange("p (m r) -> p r m", r=dil)[:, r, base:base + tot]
                        if gi == 0:
                            cp("act", dstn, psn[:, 0:tot], [spsn], [s_num])
                            cp("dve", dstd, psd[:, 0:tot], [spsd], [s_den])
                        else:
                            tt("dve", dstn, psn[:, 0:tot], dstn, ALU.add, [spsn, s_num], [s_num])
                            tt("dve", dstd, psd[:, 0:tot], dstd, ALU.add, [spsd, s_den], [s_den])
            P.op("dve", lambda e: e.reciprocal(out=den, in_=den), reads=[s_den], writes=[s_den])
            tt("dve", y_c[:, hp, :], num, den, ALU.mult, [s_num, s_den], [s_yc])
            k.dump(f"num{hp}", num, s_num, [128, S], F32)

    def pooling(l):
        qa = Alloc(k, XR, 64 * KB)
        W_ = S + 24
        ub, s_ub = qa("ub", W_, F32)
        bufs = [qa(f"pb{i}", W_, F32) for i in range(2)]
        pT, s_pT = qa("pT", S, BF16)
        e8, s_e8 = qa("e8", 16, F32)
        for t_, s_ in [(ub, s_ub)] + bufs:
            memset("pool", t_[:, 0:16], 0.0, [s_])
            memset("pool", t_[:, S + 16:S + 24], 0.0, [s_])
        for g in range(4):
            w = (2, 4, 8, 16)[g]
            Wu, sWu = load_chunk(l, ("pool", "u", g))
            Wp, sWp = load_chunk(l, ("pool", "w", g))
            for j in range(NT):
                ps, sps = k.next_ps()
                for c in range(8):
                    mm(ps[:], Wu[:, c, :], hT_full[:, c, j * 512:(j + 1) * 512], c == 0, c == 7, [sWu, s_hTf], [sps])
                cp("act", ub[:, 16 + j * 512:16 + (j + 1) * 512], ps[:], [sps], [s_ub])
            src, ssrc = ub, s_ub
            kk_ = 1
            bi = 0
            while kk_ < w:
                dst, sdst = bufs[bi % 2]
                bi += 1
                eng = "pool" if bi % 2 == 0 else "dve"
                tt(eng, dst[:, 16:S + 24], src[:, 16:S + 24], src[:, 16 - kk_:S + 24 - kk_], ALU.add, [ssrc], [sdst])
                src, ssrc = dst, sdst
                kk_ *= 2
            sh = 16 + w // 2 - 1
            stt(pT, src[:, sh:sh + S], 1.0 / w, ub[:, 16:16 + S], ALU.mult, ALU.subtract, [ssrc, s_ub], [s_pT])
            pe_o = CF["pedge"][0] + g * 16
            tt("dve", e8[:, 0:8], src[:, sh:sh + 8], cf_t[:, pe_o:pe_o + 8], ALU.mult, [ssrc, s_cf], [s_e8])
            tt("dve", e8[:, 8:16], src[:, sh + S - 8:sh + S], cf_t[:, pe_o + 8:pe_o + 16], ALU.mult, [ssrc, s_cf], [s_e8])
            tt("dve", pT[:, 0:8], e8[:, 0:8], ub[:, 16:24], ALU.subtract, [s_e8, s_ub, s_pT], [s_pT])
            tt("dve", pT[:, S - 8:S], e8[:, 8:16], ub[:, 16 + S - 8:16 + S], ALU.subtract, [s_e8, s_ub, s_pT], [s_pT])
            k.dump(f"pT{g}", pT, s_pT, [128, S], BF16)
            for j in range(NT):
                ps, sps = k.next_ps()
                mm(ps[:], Wp[:, 0, :], pT[:, j * 512:(j + 1) * 512], True, True, [sWp, s_pT], [sps])
                act(y_b[:, g, j * 512:(j + 1) * 512], ps[:], AF.Copy, [sps, s_pp], [s_yb], scale=ppv(l, "pool_scale", g))

    def merge(l):
        for j in range(NT):
            cols = slice(j * 512, (j + 1) * 512)
            src = k.xs_d.rearrange("p (c t) -> p c t", c=8)[:, :, cols]
            dma(xT[:, :, cols], src, [s_xs[j]], [s_x[j]], s_x[j])
            for d in range(8):
                for bi, (pk, ysrc, sy, nk) in enumerate((("pa", y_a, s_ya, 4), ("pb", y_b, s_yb, 4), ("pc", y_c, s_yc, 2))):
                    Wg, sWg = load_chunk(l, ("gate", bi * 8 + d))
                    Wp, sWp = load_chunk(l, (pk, d))
                    psg, spsg = k.next_ps()
                    for c in range(8):
                        mm(psg[:], Wg[:, c, :], hT_full[:, c, cols], c == 0, c == 7, [sWg, s_hTf], [spsg])
                    psy, spsy = k.next_ps()
                    for c in range(nk):
                        mm(psy[:], Wp[:, c, :], ysrc[:, c, cols], c == 0, c == nk - 1, [sWp, sy], [spsy])
                    g_, sg_ = gate_t[bi % 2]
                    act(g_, psg[:], AF.Sigmoid, [spsg, s_pp], [sg_], bias=ppv(l, "b_gate", bi * 8 + d))
                    if bi == 0:
                        tt("dve", m32[:, d, :], psy[:], g_, ALU.mult, [spsy, sg_], [s_m32])
                    else:
                        t_, st_ = gt2_t[bi % 2]
                        tt("dve", t_, psy[:], g_, ALU.mult, [spsy, sg_], [st_])
                        if bi == 1:
                            tt("pool", m32[:, d, :], m32[:, d, :], t_, ALU.add, [s_m32, st_], [s_m32])
                        else:
                            tt("pool", mb[:, d, :], m32[:, d, :], t_, ALU.add, [s_m32, st_], [s_mb])
            for d2 in range(8):
                Wo, sWo = load_chunk(l, ("wo", d2))
                ps, sps = k.next_ps()
                for c in range(8):
                    mm(ps[:], Wo[:, c, :], mb[:, c, :], c == 0, c == 7, [sWo, s_mb], [sps])
                tt("dve", xT[:, d2, cols], ps[:], xT[:, d2, cols], ALU.add, [sps, s_x[j]], [s_x[j]])

    def mixer(l):
        for j in range(NT):
            rmsnorm_tile(j, lambda c: ppv(l, "mix_norm", c), s_pp, hT_full[:, :, j * 512:(j + 1) * 512], s_hTf)
        for j in range(NT):
            cols = slice(j * 512, (j + 1) * 512)
            dst = k.xs_d.rearrange("p (c t) -> p c t", c=8)[:, :, cols]
            dma(dst, xT[:, :, cols], [s_x[j]], [s_xs[j]], s_xs[j])
        P.barrier()
        if "m_norm" in k.stages:
            return
        if "dn" in k.stages or "mix" in k.stages:
            deltanet(l)
        else:
            memset("pool", ya_flat, 0.0, [s_ya])
        k.dump("ya", ya_flat, s_ya, [128, 4 * S], BF16)
        P.barrier()
        if "da" in k.stages or "mix" in k.stages:
            attention(l)
        else:
            memset("pool", yc_flat, 0.0, [s_yc])
        k.dump("yc", yc_flat, s_yc, [128, 2 * S], BF16)
        P.barrier()
        if "pool" in k.stages or "mix" in k.stages:
            pooling(l)
        else:
            memset("pool", yb_flat, 0.0, [s_yb])
        k.dump("yb", yb_flat, s_yb, [128, 4 * S], BF16)
        P.barrier()
        if "m_set" in k.stages:
            for j in range(NT):
                cols = slice(j * 512, (j + 1) * 512)
                src = k.xs_d.rearrange("p (c t) -> p c t", c=8)[:, :, cols]
                dma(xT[:, :, cols], src, [s_xs[j]], [s_x[j]], s_x[j])
            return
        merge(l)
        P.barrier()

    has_mix = any(s_ in stages for s_ in ("mix", "dn", "da", "pool", "mrg", "m_norm", "m_set"))
    for s in range(nseq):
        for j in range(NT):
            cols = slice(j * 512, (j + 1) * 512)
            src = k.x_in[s].rearrange("p (c t) -> p c t", c=8)[:, :, cols]
            dma(xT[:, :, cols], src, [], [s_x[j]], s_x[j])
        for l in range(nlayers):
            if "ffn1" in stages:
                ffn(l, "ffn1")
            if has_mix:
                P.barrier()
                mixer(l)
            if "ffn2" in stages:
                ffn(l, "ffn2")
        for j in range(NT):
            cols = slice(j * 512, (j + 1) * 512)
            rmsnorm_tile(j, lambda c: fin_t[:, c:c + 1], s_fin, xT[:, :, cols], s_x[j])
            dst = k.out[s].rearrange("p (c t) -> p c t", c=8)[:, :, cols]
            dma(dst, xT[:, :, cols], [s_x[j]], [], s_x[j])
        if has_mix:
            P.barrier()
    P.emit()
    return k


_CONSTS = None


def _prep_inputs(inputs):
    global _CONSTS
    inp = {n: np.asarray(v) for n, v in inputs.items()}
    wf = _host_weights(inp).reshape(NL, WTOT_PAD // WROW, WROW)
    pp, fin = _host_params(inp)
    pp2 = np.ascontiguousarray(pp.transpose(1, 0, 2).reshape(128, NL * NPP))
    if _CONSTS is None:
        _CONSTS = _host_consts()
    cf, cb, rope = _CONSTS
    return inp, dict(wf=wf, pp=pp2, fin=fin, cf=cf, cb=cb, rope=rope)


def _x_to_dev(xs):
    n = xs.shape[0]
    return np.ascontiguousarray(xs.reshape(n, S, 8, 128).transpose(0, 3, 2, 1).reshape(n, 128, 8 * S))


def _x_from_dev(o):
    n = o.shape[0]
    return np.ascontiguousarray(o.reshape(n, 128, 8, S).transpose(0, 3, 2, 1).reshape(n, S, D))


NSEQ_LAUNCH = 4


def kernel(**inputs):
    inp, shared = _prep_inputs(inputs)
    x = inp["x"]
    k = build(nseq=NSEQ_LAUNCH)
    out = np.empty((NCORES * NSEQ, S, D), np.float32)
    for r in range(NSEQ // NSEQ_LAUNCH):
        in_maps = []
        for c in range(NCORES):
            m = dict(shared)
            b0 = c * NSEQ + r * NSEQ_LAUNCH
            m["x_in"] = _x_to_dev(x[b0:b0 + NSEQ_LAUNCH])
            in_maps.append(m)
        res = run_bass_kernel_spmd(k.nc, in_maps, core_ids=list(range(NCORES)))
        for c in range(NCORES):
            b0 = c * NSEQ + r * NSEQ_LAUNCH
            out[b0:b0 + NSEQ_LAUNCH] = _x_from_dev(np.asarray(res.results[c]["out"]))
    return out
```
